# Optimizing a Trainium2 kernel written in Bass

```python
import jax, jax.numpy as jnp
from jax import lax
import numpy as np

D_MODEL = 2048
BATCH = 1
SEQ = 8192
DEPTH = 4

N_A = DEPTH // 2
N_B = DEPTH - N_A
MIX = 3 * D_MODEL // 4
MEM_WIDTH = D_MODEL - MIX
POOL_WINDOWS = (2, 4, 8, 16)
N_POOL_GROUPS = len(POOL_WINDOWS)
POOL_CG = MIX // N_POOL_GROUPS
HEAD_DIM = 64
N_Q_HEADS = MIX // HEAD_DIM
N_KV_HEADS = 3
GQA_GROUP = N_Q_HEADS // N_KV_HEADS
WINDOW = 128
BLOCK = 128
MEM_LEN = 256
MEM_HEADS = 4
MEM_HEAD_DIM = MEM_WIDTH // MEM_HEADS
D_FF = 5632
ROPE_THETA = 10000.0
EPS = 1e-5

kernel_name = "yoco_pool_swa_sink_macaron_mem"


def rmsnorm(x, g):
    xf = x.astype(jnp.float32)
    y = xf * lax.rsqrt(jnp.mean(xf * xf, axis=-1, keepdims=True) + EPS)
    return (y * g.astype(jnp.float32)).astype(x.dtype)


def swiglu(x, w_gate, w_up, w_down):
    return (jax.nn.silu(x @ w_gate) * (x @ w_up)) @ w_down


def rope(x, positions):
    hd = x.shape[-1]
    half = hd // 2
    inv_freq = 1.0 / (ROPE_THETA ** (jnp.arange(half, dtype=jnp.float32) * (2.0 / hd)))
    ang = positions.astype(jnp.float32)[..., None] * inv_freq
    cos = jnp.cos(ang)[:, :, None, :]
    sin = jnp.sin(ang)[:, :, None, :]
    xf = x.astype(jnp.float32)
    x1, x2 = xf[..., :half], xf[..., half:]
    out = jnp.concatenate([x1 * cos - x2 * sin, x2 * cos + x1 * sin], axis=-1)
    return out.astype(x.dtype)


def pool_mixer(u, w_grp, scale):
    B, S, C = u.shape
    uf = u.astype(jnp.float32).reshape(B, S, N_POOL_GROUPS, POOL_CG)
    cs = jnp.cumsum(uf, axis=1)
    t1 = jnp.arange(S, dtype=jnp.float32) + 1.0
    pooled = []
    for g, w in enumerate(POOL_WINDOWS):
        c = cs[:, :, g]
        shifted = jnp.pad(c, ((0, 0), (w, 0), (0, 0)))[:, :S]
        cnt = jnp.minimum(t1, float(w))[None, :, None]
        pooled.append((c - shifted) / cnt)
    diff = (jnp.stack(pooled, axis=2) - uf).astype(u.dtype)
    y = jnp.einsum('bsgc,gcd->bsgd', diff, w_grp).reshape(B, S, C)
    return y * scale


def swa_sink_attention(q, k, v, sinks):
    B, S, _, hd = q.shape
    nb = S // BLOCK
    qb = q.reshape(B, nb, BLOCK, N_KV_HEADS, GQA_GROUP, hd)
    kb = k.reshape(B, nb, BLOCK, N_KV_HEADS, hd)
    vb = v.reshape(B, nb, BLOCK, N_KV_HEADS, hd)
    pad = ((0, 0), (1, 0), (0, 0), (0, 0), (0, 0))
    kk = jnp.concatenate([jnp.pad(kb, pad)[:, :nb], kb], axis=2)
    vv = jnp.concatenate([jnp.pad(vb, pad)[:, :nb], vb], axis=2)
    s = jnp.einsum('bnqkgd,bnjkd->bnkgqj', qb, kk).astype(jnp.float32) * (hd ** -0.5)
    qi = jnp.arange(BLOCK)[:, None] + BLOCK
    kj = jnp.arange(2 * BLOCK)[None, :]
    rel = qi - kj
    band = (rel >= 0) & (rel < WINDOW)
    first = (jnp.arange(nb)[:, None, None] > 0) | (kj[None] >= BLOCK)
    valid = band[None] & first
    s = jnp.where(valid[None, :, None, None], s, -jnp.inf)
    sk = sinks.astype(jnp.float32).reshape(N_KV_HEADS, GQA_GROUP)[None, None, :, :, None]
    m = jnp.maximum(jnp.max(s, axis=-1), sk)
    p = jnp.exp(s - m[..., None])
    denom = jnp.sum(p, axis=-1) + jnp.exp(sk - m)
    p = (p / denom[..., None]).astype(v.dtype)
    o = jnp.einsum('bnkgqj,bnjkd->bnqkgd', p, vv)
    return o.reshape(B, S, N_Q_HEADS * hd)


def mem_attention(qm, mk, mv):
    B, S, _ = qm.shape
    q = qm.reshape(B, S, MEM_HEADS, MEM_HEAD_DIM)
    k = mk.reshape(B, -1, MEM_HEADS, MEM_HEAD_DIM)
    v = mv.reshape(B, -1, MEM_HEADS, MEM_HEAD_DIM)
    s = jnp.einsum('bshd,bmhd->bhsm', q, k).astype(jnp.float32) * (MEM_HEAD_DIM ** -0.5)
    p = jax.nn.softmax(s, axis=-1).astype(v.dtype)
    return jnp.einsum('bhsm,bmhd->bshd', p, v).reshape(B, S, MEM_WIDTH)


def setup_inputs(seed: int = 0) -> dict:
    key = jax.random.key(seed)
    ks = jax.random.split(key, 20)
    f32 = jnp.float32
    nrm = lambda k, shape, fan_in: jax.random.normal(k, shape, f32) * (fan_in ** -0.5)
    x = jax.random.normal(ks[0], (BATCH, SEQ, D_MODEL), f32)
    mem = jax.random.normal(ks[1], (BATCH, MEM_LEN, D_MODEL), f32)
    offset = jax.random.randint(ks[2], (BATCH, 1), 0, 1024, dtype=jnp.int32)
    positions = offset + jnp.arange(SEQ, dtype=jnp.int32)[None, :]
    norms = 1.0 + 0.02 * jax.random.normal(ks[3], (DEPTH, 3, D_MODEL), f32)
    ffn_gate = nrm(ks[4], (DEPTH, 2, D_MODEL, D_FF), D_MODEL)
    ffn_up = nrm(ks[5], (DEPTH, 2, D_MODEL, D_FF), D_MODEL)
    ffn_down = nrm(ks[6], (DEPTH, 2, D_FF, D_MODEL), D_FF)
    w_in = nrm(ks[7], (DEPTH, D_MODEL, MIX + MEM_WIDTH), D_MODEL)
    w_out = nrm(ks[8], (DEPTH, MIX + MEM_WIDTH, D_MODEL), MIX + MEM_WIDTH)
    pool_w = nrm(ks[9], (N_A, N_POOL_GROUPS, POOL_CG, POOL_CG), POOL_CG)
    pool_scale = 1.0 + 0.1 * jax.random.normal(ks[10], (N_A, MIX), f32)
    w_kv = nrm(ks[11], (D_MODEL, 2 * N_KV_HEADS * HEAD_DIM), D_MODEL)
    kv_norm = 1.0 + 0.02 * jax.random.normal(ks[12], (D_MODEL,), f32)
    sinks = 0.5 * jax.random.normal(ks[13], (N_B, N_Q_HEADS), f32)
    w_mem_kv = nrm(ks[14], (DEPTH, D_MODEL, 2 * MEM_WIDTH), D_MODEL)
    mem_norm = 1.0 + 0.02 * jax.random.normal(ks[15], (D_MODEL,), f32)
    final_norm = 1.0 + 0.02 * jax.random.normal(ks[16], (D_MODEL,), f32)
    return {"x": x, "mem": mem, "positions": positions, "norms": norms,
            "ffn_gate": ffn_gate, "ffn_up": ffn_up, "ffn_down": ffn_down,
            "w_in": w_in, "w_out": w_out, "pool_w": pool_w, "pool_scale": pool_scale,
            "w_kv": w_kv, "kv_norm": kv_norm, "sinks": sinks, "w_mem_kv": w_mem_kv,
            "mem_norm": mem_norm, "final_norm": final_norm}


def reference(x, mem, positions, norms, ffn_gate, ffn_up, ffn_down, w_in, w_out,
              pool_w, pool_scale, w_kv, kv_norm, sinks, w_mem_kv, mem_norm, final_norm):
    B, S, _ = x.shape
    h = x
    mem_n = rmsnorm(mem, mem_norm)
    k_shared = None
    v_shared = None
    for l in range(DEPTH):
        h = h + 0.5 * swiglu(rmsnorm(h, norms[l, 0]), ffn_gate[l, 0], ffn_up[l, 0], ffn_down[l, 0])
        a = rmsnorm(h, norms[l, 1]) @ w_in[l]
        u, qm = a[..., :MIX], a[..., MIX:]
        if l < N_A:
            y = pool_mixer(u, pool_w[l], pool_scale[l])
        else:
            q = rope(u.reshape(B, S, N_Q_HEADS, HEAD_DIM), positions)
            y = swa_sink_attention(q, k_shared, v_shared, sinks[l - N_A])
        mkv = mem_n @ w_mem_kv[l]
        ym = mem_attention(qm, mkv[..., :MEM_WIDTH], mkv[..., MEM_WIDTH:])
        h = h + jnp.concatenate([y, ym], axis=-1) @ w_out[l]
        h = h + 0.5 * swiglu(rmsnorm(h, norms[l, 2]), ffn_gate[l, 1], ffn_up[l, 1], ffn_down[l, 1])
        if l == N_A - 1:
            kv = rmsnorm(h, kv_norm) @ w_kv
            kvw = N_KV_HEADS * HEAD_DIM
            k_shared = rope(kv[..., :kvw].reshape(B, S, N_KV_HEADS, HEAD_DIM), positions)
            v_shared = kv[..., kvw:].reshape(B, S, N_KV_HEADS, HEAD_DIM)
    return rmsnorm(h, final_norm)
```

```python
import math
from contextlib import ExitStack
import numpy as np
import concourse.bass as bass
import concourse.mybir as mybir
from concourse.bass_utils import run_bass_kernel_spmd

F32, BF16, I32 = mybir.dt.float32, mybir.dt.bfloat16, mybir.dt.int32
AF = mybir.ActivationFunctionType
ALU = mybir.AluOpType
AX = mybir.AxisListType

NCORES = 8
D = 2048
DFF = 5632
SEQ = 8192
TOK = SEQ // NCORES
HALO = 160
TB = TOK + HALO
NKEY = TOK + 128
KOFF = TB - NKEY
NCH = 16
NFF = DFF // 128
GSZ = 4
NGRP = NFF // GSZ
MIXC = 12
R = 6
EPS = 1e-5
DEPTH = 4
N_A = 2
MASKV = -30000.0
USE_POW = False
TWO_PI = 2.0 * math.pi

C_GAIN = 0
C_PSC = 240
C_SINK = 264
C_INVF = 312
C_MASK = 313
C_ID = 569
C_ROT = 697
C_N = 825
CC_INVC = 0
CC_VALID = 64
CC_MASK0 = 65
CC_N = 321


class Prog:
    ENGS = ("pe", "act", "dve", "pool", "sp")

    def __init__(self, nc, es, plan, ring, wsrc):
        self.nc, self.es = nc, es
        self.real = plan is not None
        self.plan = plan if plan is not None else []
        self.ring, self.wsrc = ring, wsrc
        self.wi = 0
        self.pin = None
        self.wissued = 0
        self.streams = {e: [] for e in self.ENGS}
        self.sem, self.cnt = {}, {}
        self.lastw, self.rd = {}, {}
        self.seen = {e: {} for e in self.ENGS}
        self.rotc = {}
        self.nmisc = 0
        self.final = []

    def rot(self, key, banks):
        i = self.rotc.get(key, 0)
        self.rotc[key] = i + 1
        return banks[i % len(banks)]

    def getsem(self, name):
        if name not in self.sem:
            self.sem[name] = self.es.enter_context(self.nc.semaphore(name))
            self.cnt[name] = 0
        return self.sem[name]

    def op(self, eng, fn, r=(), w=(), dsem=None):
        if not self.real:
            return
        deps = {}

        def need(t):
            if t is not None and deps.get(t[0], 0) < t[1]:
                deps[t[0]] = t[1]
        w = list(w)
        if dsem is not None:
            w.append(("sem", dsem))
        for x in r:
            need(self.lastw.get(x))
        for x in w:
            need(self.lastw.get(x))
            for s, v in self.rd.get(x, {}).items():
                need((s, v))
        st = self.streams[eng]
        for s, v in deps.items():
            if eng == "pe" and s == "S_pe":
                continue
            if self.seen[eng].get(s, 0) >= v:
                continue
            self.seen[eng][s] = v
            st.append(("w", s, v))
        sname, inc = (dsem, 16) if dsem is not None else ("S_" + eng, 1)
        self.getsem(sname)
        self.cnt[sname] += inc
        tk = (sname, self.cnt[sname])
        st.append(("o", fn, sname, inc))
        for x in w:
            self.lastw[x] = tk
            self.rd[x] = {}
        for x in r:
            dd = self.rd.setdefault(x, {})
            if dd.get(sname, 0) < tk[1]:
                dd[sname] = tk[1]

    def dma(self, out, in_, r=(), w=(), eng="sp"):
        name = "m%d" % (self.nmisc % 16)
        self.nmisc += 1
        self.op(eng, [("dma_start", dict(out=out, in_=in_))], r=r, w=w, dsem=name)

    def wtile(self, src):
        i = self.wi
        self.wi += 1
        if not self.real:
            self.plan.append(src)
            return None, None
        assert self.plan[i] == src, (i, self.plan[i], src)
        limit = i + R - 3
        if self.pin is not None:
            limit = min(limit, self.pin + R - 1)
        while self.wissued < len(self.plan) and self.wissued <= limit:
            j = self.wissued
            self.wissued += 1
            sl = j % R
            kind, idx = self.plan[j][0], self.plan[j][1:]
            srcap = self.wsrc[kind][idx]
            dst = self.ring[:, sl, :]
            self.op("pool", [("dma_start", dict(out=dst, in_=srcap))], w=[("ring", sl)], dsem="slot%d" % sl)
        sl = i % R
        return self.ring[:, sl, :], ("ring", sl)

    def replay(self, block):
        def run(eng, name):
            for it in self.streams[name]:
                if it[0] == "w":
                    eng.wait_ge(self.sem[it[1]], it[2])
                else:
                    ins = None
                    for (mname, kw) in it[1]:
                        ins = getattr(eng, mname)(**kw)
                    ins.then_inc(self.sem[it[2]], it[3])
            if name == "sp":
                for res in self.final:
                    s, v = self.lastw[res]
                    eng.wait_ge(self.sem[s], v)
        block.sync(lambda e: run(e, "sp"))
        block.gpsimd(lambda e: run(e, "pool"))
        block.scalar(lambda e: run(e, "act"))
        block.vector(lambda e: run(e, "dve"))
        block.tensor(lambda e: run(e, "pe"))


def A(name, **kw):
    return (name, kw)


def emit_program(P, T, nlayers, dbg):
    real = P.real
    h, xn, regB, late, ps = T["h"], T["xn"], T["regB"], T["late"], T["ps"]
    tmpA, scr, V, mkT, mv = T["tmpA"], T["scr"], T["V"], T["mkT"], T["mv"]
    gains, psc, sinks, nsinks, invf = T["gains"], T["psc"], T["sinks"], T["nsinks"], T["invf"]
    maskbf, ident, rrot, ones, invc, valid, epst = T["maskbf"], T["ident"], T["rrot"], T["ones"], T["invc"], T["valid"], T["epst"]
    att = T["att"]
    B32, BI32 = T["B32"], T["BI32"]
    cosT, sinT, kT = T["cosT"], T["sinT"], T["kT"]
    PU, PA, PB = T["PU"], T["PA"], T["PB"]
    memn = T["memn"]

    TT_ALL = [(1, HALO, HALO + 512), (2, HALO + 512, TB), (0, 0, HALO)]
    TT_MAIN = TT_ALL[:2]
    ALLB = [("B", c, ti) for c in range(16) for ti in range(3)]
    LATE = ["late0", "late1", "late2"]
    PSB = lambda k: ("ps", k)
    PROJ = [0, 1, 2, 3, 4, 5]

    def mm_group(out_ap, pairs, r, w):
        n = len(pairs)
        P.op("pe", [A("matmul", out=out_ap, lhsT=a, rhs=b, start=(i == 0), stop=(i == n - 1)) for i, (a, b) in enumerate(pairs)],
             r=r, w=w)

    if real:
        xT, cst, ccst = T["xT"], T["cst"], T["ccst"]
        P.dma(B32[:, 0:C_N], cst, w=ALLB)
        P.dma(B32[:, C_N:C_N + CC_N], ccst, w=ALLB)
        for (ti, lo, hi) in TT_ALL:
            for q in range(4):
                P.dma(h[:, 4 * q:4 * q + 4, lo:hi], xT[4 * q * 128:(4 * q + 4) * 128, lo:hi].rearrange("(c p) t -> p c t", p=128),
                      w=[("h", c, ti) for c in range(4 * q, 4 * q + 4)])

        def cp(o, i):
            P.op("dve", [A("tensor_copy", out=o, in_=i)], r=ALLB, w=["consts"])
        cp(gains[:], B32[:, C_GAIN:C_GAIN + 240])
        cp(psc[:], B32[:, C_PSC:C_PSC + 24])
        cp(sinks[:], B32[:, C_SINK:C_SINK + 48])
        cp(invf[:], B32[:, C_INVF:C_INVF + 1])
        cp(maskbf[:, 0, :], B32[:, C_MASK:C_MASK + 256])
        cp(ident[:], B32[:, C_ID:C_ID + 128])
        cp(rrot[:], B32[:, C_ROT:C_ROT + 128])
        cp(invc[:], B32[:, C_N + CC_INVC:C_N + CC_INVC + 64])
        cp(valid[:], B32[:, C_N + CC_VALID:C_N + CC_VALID + 1])
        cp(maskbf[:, 1, :], B32[:, C_N + CC_MASK0:C_N + CC_MASK0 + 256])
        P.op("dve", [A("tensor_scalar", out=nsinks[:], in0=sinks[:], scalar1=-1.0, scalar2=None, op0=ALU.mult)], w=["consts"])
        P.op("dve", [A("memset", ap=ones[:], constant=1.0)], w=["consts"])
        P.op("dve", [A("memset", ap=epst[:], constant=EPS)], w=["consts"])

    def rmsnorm(gi, tiles, src, src_res, dst, dst_res, nchunk=16, inv_n=1.0 / D, dve_help=True):
        for idx, (ti, lo, hi) in enumerate(tiles):
            n = hi - lo
            bank = P.rot("n", [6, 7])
            if not real:
                continue
            pn = ps[bank][:, :n]
            for c in range(nchunk):
                sb = c % 2
                sqb = scr[:, sb * 512: sb * 512 + n]
                if dve_help and idx == 0 and sb == 1:
                    P.op("dve", [A("tensor_tensor", out=sqb, in0=src[:, c, lo:hi], in1=src[:, c, lo:hi], op=ALU.mult)],
                         r=src_res(c, ti), w=[("P", sb)])
                else:
                    P.op("act", [A("activation", out=sqb, in_=src[:, c, lo:hi], func=AF.Square)],
                         r=src_res(c, ti), w=[("P", sb)])
                P.op("pe", [A("matmul", out=pn, lhsT=ones[:], rhs=sqb, start=(c == 0), stop=(c == nchunk - 1))],
                     r=[("P", sb), "consts"], w=[PSB(bank)])
            if USE_POW:
                P.op("dve", [A("tensor_scalar", out=pn, in0=pn, scalar1=inv_n, scalar2=EPS, op0=ALU.mult, op1=ALU.add)],
                     w=[PSB(bank)])
                P.op("dve", [A("tensor_scalar", out=pn, in0=pn, scalar1=-0.5, scalar2=None, op0=ALU.pow)], w=[PSB(bank)])
            else:
                P.op("act", [A("activation", out=pn, in_=pn, func=AF.Sqrt, bias=epst[:, 0:1], scale=inv_n)],
                     r=["consts"], w=[PSB(bank)])
                P.op("dve", [A("reciprocal", out=pn, in_=pn)], w=[PSB(bank)])
            for c in range(nchunk):
                P.op("dve", [A("scalar_tensor_tensor", out=dst[:, c, lo:hi], in0=src[:, c, lo:hi],
                               scalar=gains[:, gi * 16 + c: gi * 16 + c + 1], in1=pn, op0=ALU.mult, op1=ALU.mult)],
                     r=src_res(c, ti) + ["consts"], w=[PSB(bank)] + dst_res(c, ti))

    hres = lambda c, ti: [("h", c, ti)]
    xres = lambda c, ti: [("xn", c, ti)]

    def proj_tiles(kind, idx_fn, noc, tiles, evac, after=None, head=0):
        def one(oc, wt, wr, tl):
            for (ti, lo, hi) in tl:
                n = hi - lo
                bank = P.rot("p", PROJ)
                if real:
                    mm_group(ps[bank][:, :n], [(wt[:, kc * 128:(kc + 1) * 128], xn[:, kc, lo:hi]) for kc in range(16)],
                             r=[wr] + [("xn", kc, ti) for kc in range(16)], w=[PSB(bank)])
                    evac(oc, ti, lo, hi, bank)
        if head:
            P.pin = P.wi
            held = []
            for oc in range(head):
                wt, wr = P.wtile((kind,) + idx_fn(oc))
                held.append((oc, wt, wr))
                one(oc, wt, wr, tiles[:1])
            for (oc, wt, wr) in held:
                one(oc, wt, wr, tiles[1:])
                if real and after is not None:
                    after(oc)
            P.pin = None
        for oc in range(head, noc):
            wt, wr = P.wtile((kind,) + idx_fn(oc))
            one(oc, wt, wr, tiles)
            if real and after is not None:
                after(oc)

    def ffn(l, j, tiles, extra=()):
        extra = list(extra)
        def GU(g):
            buf = g % 2

            def one(c4, wg, rg, wu, ru, tl):
                for (ti, lo, hi) in tl:
                    n = hi - lo
                    bg = P.rot("g", [0, 1])
                    bu = P.rot("u", [2, 3])
                    if not real:
                        continue
                    xr = [("xn", kc, ti) for kc in range(16)]
                    mm_group(ps[bg][:, :n], [(wg[:, kc * 128:(kc + 1) * 128], xn[:, kc, lo:hi]) for kc in range(16)],
                             r=[rg] + xr, w=[PSB(bg)])
                    mm_group(ps[bu][:, :n], [(wu[:, kc * 128:(kc + 1) * 128], xn[:, kc, lo:hi]) for kc in range(16)],
                             r=[ru] + xr, w=[PSB(bu)])
                    P.op("act", [A("activation", out=tmpA[:, :n], in_=ps[bg][:, :n], func=AF.Silu)], w=[PSB(bg), "tmpA"])
                    hc = buf * GSZ + c4
                    P.op("dve", [A("tensor_tensor", out=regB[:, hc, lo:hi], in0=tmpA[:, :n], in1=ps[bu][:, :n], op=ALU.mult)],
                         r=["tmpA"], w=[PSB(bu), ("B", hc, ti)])

            c4s = list(range(GSZ))
            if g == 0:
                P.pin = P.wi
                held = []
                for c4 in c4s[:2]:
                    wg, rg = P.wtile(("gate", l, j, c4))
                    wu, ru = P.wtile(("up", l, j, c4))
                    held.append((c4, wg, rg, wu, ru))
                    one(c4, wg, rg, wu, ru, tiles[:1])
                for (c4, wg, rg, wu, ru) in held:
                    one(c4, wg, rg, wu, ru, tiles[1:])
                P.pin = None
                c4s = c4s[2:]
            for c4 in c4s:
                c = g * GSZ + c4
                wg, rg = P.wtile(("gate", l, j, c))
                wu, ru = P.wtile(("up", l, j, c))
                one(c4, wg, rg, wu, ru, tiles)
                if extra and g >= 1:
                    extra.pop(0)()

        def DN(g):
            buf = g % 2
            for dq in range(4):
                wd, rdn = P.wtile(("down", l, j, g, dq))
                for dcl in range(4):
                    dc = dq * 4 + dcl
                    for (ti, lo, hi) in tiles:
                        n = hi - lo
                        bd = P.rot("d", [4, 5, 6, 7])
                        if not real:
                            continue
                        mm_group(ps[bd][:, :n],
                                 [(wd[:, c4 * 512 + dcl * 128: c4 * 512 + (dcl + 1) * 128], regB[:, buf * GSZ + c4, lo:hi])
                                  for c4 in range(GSZ)],
                                 r=[rdn] + [("B", buf * GSZ + c4, ti) for c4 in range(GSZ)], w=[PSB(bd)])
                        P.op("dve", [A("scalar_tensor_tensor", out=h[:, dc, lo:hi], in0=ps[bd][:, :n], scalar=0.5,
                                       in1=h[:, dc, lo:hi], op0=ALU.mult, op1=ALU.add)], w=[PSB(bd), ("h", dc, ti)])

        if j == 0:
            mem_load(l)
        rmsnorm(l * 3 + (0 if j == 0 else 2), tiles, h, hres, xn, xres)
        GU(0)
        for g in range(NGRP):
            if g + 1 < NGRP:
                GU(g + 1)
            DN(g)
            if j == 0 and g == 1:
                mem_norm(l)
            if j == 0 and g == 3:
                mem_mm(l)
        while extra:
            extra.pop(0)()

    def attention_jobs(jobs):
        nj = len(jobs)
        SBK = [0, 1, 2, 3]
        TBK = [4, 5]

        def sA(k):
            jb = jobs[k]
            jb["sbank"] = SBK[k % 4]
            jb["emit_S"](jb["sbank"])

        def sB1(k):
            jb = jobs[k]
            bank, nq, sb, sc = jb["sbank"], jb["nq"], k % 3, jb["scale"]
            st = att["stat"][0:nq, sb * 16:(sb + 1) * 16]
            s3 = ps[bank][0:nq, 0:512].rearrange("p (h j) -> p h j", h=2)
            P.op("dve", [A("tensor_reduce", out=st[:, 0:2], in_=s3, axis=AX.X, op=ALU.max)], r=[PSB(bank)], w=[("st", sb)])
            if jb["nsink"] is not None:
                P.op("dve", [A("scalar_tensor_tensor", out=st[:, 2:4], in0=st[:, 0:2], scalar=-sc, in1=jb["nsink"],
                               op0=ALU.mult, op1=ALU.min)], r=["consts"], w=[("st", sb)])
            else:
                P.op("dve", [A("tensor_scalar", out=st[:, 2:4], in0=st[:, 0:2], scalar1=-sc, scalar2=None, op0=ALU.mult)],
                     w=[("st", sb)])

        def sB2(k):
            jb = jobs[k]
            bank, nq, sb, sc = jb["sbank"], jb["nq"], k % 3, jb["scale"]
            st = att["stat"][0:nq, sb * 16:(sb + 1) * 16]
            Pm = att["P"][0:nq, sb, :]
            ins = [A("activation", out=Pm[:, hh * 256:(hh + 1) * 256], in_=ps[bank][0:nq, hh * 256:(hh + 1) * 256], func=AF.Exp,
                     bias=st[:, 2 + hh:3 + hh], scale=sc, accum_out=st[:, 4 + hh:5 + hh]) for hh in range(2)]
            if jb["sink"] is not None:
                ins += [A("activation", out=st[:, 6 + hh:7 + hh], in_=jb["sink"][:, hh:hh + 1], func=AF.Exp,
                          bias=st[:, 2 + hh:3 + hh], scale=1.0) for hh in range(2)]
            P.op("act", ins, r=[PSB(bank), ("st", sb), "consts"], w=[("P", sb), ("st2", sb)])

        def sB3(k):
            jb = jobs[k]
            nq, sb = jb["nq"], k % 3
            st = att["stat"][0:nq, sb * 16:(sb + 1) * 16]
            if jb["sink"] is not None:
                P.op("dve", [A("tensor_tensor", out=st[:, 8:10], in0=st[:, 4:6], in1=st[:, 6:8], op=ALU.add)],
                     r=[("st2", sb)], w=[("st3", sb)])
                P.op("dve", [A("reciprocal", out=st[:, 10:12], in_=st[:, 8:10])], w=[("st3", sb)])
            else:
                P.op("dve", [A("reciprocal", out=st[:, 10:12], in_=st[:, 4:6])], r=[("st2", sb)], w=[("st3", sb)])
            P.op("dve", [A("tensor_scalar", out=att["D"][0:nq, sb, hh, 0:nq], in0=ident[0:nq, 0:nq], scalar1=st[:, 10 + hh:11 + hh],
                           scalar2=None, op0=ALU.mult) for hh in range(2)], r=[("st3", sb), "consts"], w=[("D", sb)])

        def sC(k):
            jb = jobs[k]
            nq, sb, sp = jb["nq"], k % 3, k % 2
            bank = TBK[k % 2]
            Pm = att["P"][0:nq, sb, :]
            PT = att["PT"][:, sp, :]
            P.op("pe", [A("matmul", out=ps[bank][:, bi * 128: bi * 128 + nq], lhsT=Pm[:, bi * 128:(bi + 1) * 128],
                          rhs=att["D"][0:nq, sb, bi // 2, 0:nq], start=True, stop=True) for bi in range(4)],
                 r=[("P", sb), ("D", sb)], w=[PSB(bank)])
            if nq == 128:
                P.op("act", [A("activation", out=PT[:, 0:512], in_=ps[bank][:, 0:512], func=AF.Identity)],
                     r=[PSB(bank)], w=[("PT", sp)])
            else:
                P.op("act", [A("activation", out=PT[:, bi * 128: bi * 128 + nq], in_=ps[bank][:, bi * 128: bi * 128 + nq],
                               func=AF.Identity) for bi in range(4)], r=[PSB(bank)], w=[("PT", sp)])

        def sC2(k):
            jb = jobs[k]
            sp = k % 2
            jb["emit_PV"](att["PT"][:, sp, :], ("PT", sp))

        for t in range(nj + 5):
            if t < nj:
                sA(t)
            if 0 <= t - 1 < nj:
                sB1(t - 1)
            if 0 <= t - 2 < nj:
                sB2(t - 2)
            if 0 <= t - 3 < nj:
                sB3(t - 3)
            if 0 <= t - 4 < nj:
                sC(t - 4)
            if 0 <= t - 5 < nj:
                sC2(t - 5)

    MEMB = [("B", c, ti) for c in range(8, 16) for ti in range(3)]
    ALLB_ = MEMB

    def mem_load(l):
        if real:
            P.dma(T["m3b"], T["memT"].rearrange("(c p) m -> p c m", p=128), w=MEMB, eng="pool")

    def mem_norm(l):
        if real:
            rmsnorm(13, [(0, 0, 256)], T["m3b"], lambda c, ti: MEMB, memn, lambda c, ti: MEMB, dve_help=False)
        else:
            rmsnorm(13, [(0, 0, 256)], None, None, None, None)

    def mem_mm(l):
        for hm in range(4):
            wt, wr = P.wtile(("memk", l, hm))
            bank = P.rot("p", PROJ)
            if real:
                mm_group(ps[bank][:, 0:256], [(wt[:, kc * 128:(kc + 1) * 128], memn[:, kc, :]) for kc in range(16)],
                         r=[wr] + MEMB, w=[PSB(bank)])
                P.op("act", [A("activation", out=mkT[:, hm, :], in_=ps[bank][:, 0:256], func=AF.Identity)],
                     w=[PSB(bank), ("mkT",)])
        b0 = P.rot("p", PROJ)
        b1 = P.rot("p", PROJ)
        for t4 in range(4):
            wt, wr = P.wtile(("memv", l, t4))
            if real:
                for mc, bank in ((0, b0), (1, b1)):
                    P.op("pe", [A("matmul", out=ps[bank][:, :], lhsT=memn[:, t4 * 4 + kk, mc * 128:(mc + 1) * 128],
                                  rhs=wt[:, kk * 512:(kk + 1) * 512], start=(t4 * 4 + kk == 0), stop=(t4 * 4 + kk == 15))
                                for kk in range(4)], r=[wr] + MEMB, w=[PSB(bank)])
        if real:
            for mc, bank in ((0, b0), (1, b1)):
                P.op("act", [A("activation", out=mv[:, mc, :], in_=ps[bank][:, :], func=AF.Identity)],
                     w=[PSB(bank), ("mv",)])

    def mem_attention(tiles):
        jobs = []
        scale = 1.0 / math.sqrt(128.0)
        for (ti, lo, hi) in tiles:
            for q0 in range(lo, hi, 128):
                q1 = min(q0 + 128, hi)
                nq = q1 - q0
                for hm in (0, 2):
                    def emit_S(bank, hm=hm, q0=q0, q1=q1, nq=nq, ti=ti):
                        P.op("pe", [A("matmul", out=ps[bank][0:nq, hh * 256:(hh + 1) * 256], lhsT=regB[:, 12 + hm + hh, q0:q1],
                                      rhs=mkT[:, hm + hh, :], start=True, stop=True) for hh in range(2)],
                             r=[("B", 12 + hm, ti), ("B", 13 + hm, ti), ("mkT",)], w=[PSB(bank)])

                    def emit_PV(PT, ptres, hm=hm, q0=q0, q1=q1, nq=nq, ti=ti):
                        bank = P.rot("o", [6, 7])
                        P.op("pe", [A("matmul", out=ps[bank][:, hh * 128: hh * 128 + nq],
                                      lhsT=mv[:, half, (hm + hh) * 128:(hm + hh + 1) * 128],
                                      rhs=PT[:, (hh * 2 + half) * 128:(hh * 2 + half) * 128 + nq],
                                      start=(half == 0), stop=(half == 1)) for hh in range(2) for half in range(2)],
                             r=[ptres, ("mv",)], w=[PSB(bank)])
                        P.op("dve", [A("tensor_copy", out=xn[:, 12 + hm + hh, q0:q1], in_=ps[bank][:, hh * 128: hh * 128 + nq])
                                     for hh in range(2)], r=[PSB(bank)], w=[("xn", 12 + hm, ti), ("xn", 13 + hm, ti)])
                    jobs.append(dict(emit_S=emit_S, emit_PV=emit_PV, scale=scale, nsink=None, sink=None, nq=nq))
        attention_jobs(jobs)

    def rope_chunk(bank, n, kb0, out_ap, out_res):
        rb = P.rot("r", [6, 7])
        P.op("act", [A("activation", out=tmpA[:, :n], in_=ps[bank][:, :n], func=AF.Identity)], w=[PSB(bank), "tmpA"])
        P.op("pe", [A("matmul", out=ps[rb][:, :n], lhsT=rrot[:], rhs=tmpA[:, :n], start=True, stop=True)],
             r=["tmpA", "consts"], w=[PSB(rb)])
        P.op("dve", [A("tensor_tensor", out=ps[rb][:, :n], in0=ps[rb][:, :n], in1=sinT[:, kb0:kb0 + n], op=ALU.mult)],
             r=LATE, w=[PSB(rb)])
        P.op("dve", [A("tensor_tensor", out=tmpA[:, :n], in0=tmpA[:, :n], in1=cosT[:, kb0:kb0 + n], op=ALU.mult)],
             r=LATE, w=["tmpA"])
        P.op("dve", [A("tensor_tensor", out=out_ap, in0=tmpA[:, :n], in1=ps[rb][:, :n], op=ALU.add)],
             w=[PSB(rb), "tmpA"] + list(out_res))

    def rope_table_ops():
        if not real:
            return []
        HB = [("B", c, ti) for c in range(8, 16) for ti in range(3)]
        b0 = 4736
        posI = BI32[:, b0:b0 + NKEY]
        kI = posI
        ang = B32[:, b0 + NKEY:b0 + 2 * NKEY]
        tr = B32[:, b0 + 2 * NKEY:b0 + 3 * NKEY]
        kF = B32[:, b0 + 3 * NKEY:b0 + 4 * NKEY]
        C1 = 6.28125
        C2 = TWO_PI - 6.28125
        CL = dict(scalar1=3.1415925, scalar2=-3.1415925, op0=ALU.min, op1=ALU.max)
        dv = lambda ins, r=(): (lambda: P.op("dve", [ins], r=list(r), w=HB))
        return [
            lambda: P.dma(posI, T["pos"], w=HB),
            dv(A("tensor_copy", out=ang, in_=posI)),
            dv(A("tensor_scalar", out=ang, in0=ang, scalar1=invf[:, 0:1], scalar2=None, op0=ALU.mult), r=["consts"]),
            dv(A("tensor_scalar", out=tr, in0=ang, scalar1=1.0 / TWO_PI, scalar2=None, op0=ALU.mult)),
            dv(A("tensor_copy", out=kI, in_=tr)),
            dv(A("tensor_copy", out=kF, in_=kI)),
            dv(A("scalar_tensor_tensor", out=tr, in0=kF, scalar=-C1, in1=ang, op0=ALU.mult, op1=ALU.add)),
            dv(A("scalar_tensor_tensor", out=tr, in0=kF, scalar=-C2, in1=tr, op0=ALU.mult, op1=ALU.add)),
            dv(A("tensor_scalar", out=ang, in0=tr, **CL)),
            lambda: P.op("act", [A("activation", out=sinT, in_=ang, func=AF.Sin)], r=HB, w=LATE),
            dv(A("tensor_scalar", out=tr, in0=tr, scalar1=math.pi / 2, scalar2=None, op0=ALU.add)),
            dv(A("tensor_scalar", out=kF, in0=tr, scalar1=math.pi, scalar2=-TWO_PI, op0=ALU.is_gt, op1=ALU.mult)),
            dv(A("tensor_tensor", out=tr, in0=tr, in1=kF, op=ALU.add)),
            dv(A("tensor_scalar", out=tr, in0=tr, **CL)),
            lambda: P.op("act", [A("activation", out=cosT, in_=tr, func=AF.Sin)], r=HB, w=LATE),
        ]

    def make_kv():
        rmsnorm(12, TT_ALL, h, hres, xn, xres)
        for kvh in range(3):
            wt, wr = P.wtile(("wk", kvh))
            for (ti, lo, hi) in (TT_ALL[0], TT_ALL[1], (0, KOFF, HALO)):
                n = hi - lo
                bank = P.rot("p", PROJ)
                if real:
                    mm_group(ps[bank][:, :n], [(wt[:, kc * 128:(kc + 1) * 128], xn[:, kc, lo:hi]) for kc in range(16)],
                             r=[wr] + [("xn", kc, ti) for kc in range(16)], w=[PSB(bank)])
                    kb0 = lo - KOFF
                    rope_chunk(bank, n, kb0, kT[:, kvh, kb0:kb0 + n], LATE + ["kT"])
        w0, r0 = P.wtile(("wv", 0))
        w1, r1 = P.wtile(("wv", 1))
        if real:
            for blk in range(9):
                b0 = KOFF + blk * 128
                bank = P.rot("p", PROJ)
                tis = sorted({0 if b < HALO else (1 if b < HALO + 512 else 2) for b in (b0, b0 + 127)})
                mm_group(ps[bank][:, 0:192],
                         [(xn[:, kc, b0:b0 + 128], (w0 if kc < 8 else w1)[:, (kc % 8) * 192:(kc % 8 + 1) * 192]) for kc in range(16)],
                         r=[r0, r1] + [("xn", kc, ti) for kc in range(16) for ti in tis], w=[PSB(bank)])
                P.op("act", [A("activation", out=V[:, blk, :], in_=ps[bank][:, 0:192], func=AF.Identity)],
                     w=[PSB(bank), ("V",)])

    def swa_attention(lb):
        jobs = []
        for qb in range(8):
            q0 = HALO + qb * 128
            ti = 1 if qb < 4 else 2
            mi = 1 if qb == 0 else 0
            for hp in range(12):
                kvh = hp // 4

                def emit_S(bank, hp=hp, q0=q0, qb=qb, kvh=kvh, ti=ti, mi=mi):
                    ins = []
                    for hh in range(2):
                        p0 = hh * 64
                        ins.append(A("matmul", out=ps[bank][:, hh * 256:(hh + 1) * 256], lhsT=regB[p0:p0 + 64, hp, q0:q0 + 128],
                                     rhs=kT[p0:p0 + 64, kvh, qb * 128: qb * 128 + 256], start=True, stop=False))
                        ins.append(A("matmul", out=ps[bank][:, hh * 256:(hh + 1) * 256], lhsT=ident[:, :], rhs=maskbf[:, mi, :],
                                     start=False, stop=True))
                    P.op("pe", ins, r=[("B", hp, ti), "kT", "consts"] + LATE, w=[PSB(bank)])

                def emit_PV(PT, ptres, hp=hp, q0=q0, qb=qb, kvh=kvh, ti=ti):
                    bank = P.rot("o", [6, 7])
                    P.op("pe", [A("matmul", out=ps[bank][hh * 64:(hh + 1) * 64, 0:128], lhsT=V[:, qb + half, kvh * 64:(kvh + 1) * 64],
                                  rhs=PT[:, (hh * 2 + half) * 128:(hh * 2 + half + 1) * 128], start=(half == 0), stop=(half == 1))
                                for hh in range(2) for half in range(2)], r=[ptres, ("V",)], w=[PSB(bank)])
                    P.op("dve", [A("tensor_copy", out=xn[:, hp, q0:q0 + 128], in_=ps[bank][:, 0:128])],
                         r=[PSB(bank)], w=[("xn", hp, ti)])
                hd = 2 * hp
                jobs.append(dict(emit_S=emit_S, emit_PV=emit_PV, nq=128, scale=0.125,
                                 nsink=nsinks[:, lb * 24 + hd: lb * 24 + hd + 2],
                                 sink=sinks[:, lb * 24 + hd: lb * 24 + hd + 2]))
        attention_jobs(jobs)

    def mixer(l, tiles):
        pool_layer = l < N_A
        rmsnorm(l * 3 + 1, tiles, h, hres, xn, xres)
        t_lo, t_hi = min(t[1] for t in tiles), max(t[2] for t in tiles)

        def evac(oc, ti, lo, hi, bank):
            n = hi - lo
            if oc >= MIXC:
                P.op("act", [A("activation", out=regB[:, oc, lo:hi], in_=ps[bank][:, :n], func=AF.Identity)],
                     w=[PSB(bank), ("B", oc, ti)])
            elif pool_layer:
                if ti == 0:
                    P.op("act", [A("activation", out=PU[:, lo:hi], in_=ps[bank][:, :n], func=AF.Identity, scale=valid[:, 0:1])],
                         r=["consts"], w=[PSB(bank), "late0"])
                else:
                    P.op("act", [A("activation", out=PU[:, lo:hi], in_=ps[bank][:, :n], func=AF.Identity)],
                         w=[PSB(bank), "late0"])
            else:
                rope_chunk(bank, n, lo - KOFF, regB[:, oc, lo:hi], [("B", oc, ti)])

        def after(oc):
            if not pool_layer or oc >= MIXC:
                return
            gi = oc // 3
            wdw = (2, 4, 8, 16)[gi]
            bufs = {"late0": PU, "late1": PA, "late2": PB}
            cur = "late0"
            sh = 1
            while sh < wdw:
                nxt = "late1" if cur != "late1" else "late2"
                s_, d_ = bufs[cur], bufs[nxt]
                P.op("dve", [A("tensor_tensor", out=d_[:, t_lo + sh:t_hi], in0=s_[:, t_lo + sh:t_hi], in1=s_[:, t_lo:t_hi - sh],
                               op=ALU.add)], r=[cur], w=[nxt])
                P.op("dve", [A("tensor_copy", out=d_[:, t_lo:t_lo + sh], in_=s_[:, t_lo:t_lo + sh])], r=[cur], w=[nxt])
                cur = nxt
                sh *= 2
            S_ = bufs[cur]
            for (ti, lo, hi) in tiles:
                P.op("dve", [A("scalar_tensor_tensor", out=regB[:, oc, lo:hi], in0=S_[:, lo:hi], scalar=1.0 / wdw,
                               in1=PU[:, lo:hi], op0=ALU.mult, op1=ALU.subtract)], r=[cur, "late0"], w=[("B", oc, ti)])
            other = "late2" if cur != "late2" else "late1"
            Tm = bufs[other]
            P.op("dve", [A("tensor_tensor", out=Tm[:, 0:16], in0=S_[:, HALO:HALO + 16], in1=invc[:, gi * 16:(gi + 1) * 16],
                           op=ALU.mult)], r=[cur, "consts"], w=[other])
            P.op("dve", [A("tensor_tensor", out=regB[:, oc, HALO:HALO + 16], in0=Tm[:, 0:16], in1=PU[:, HALO:HALO + 16],
                           op=ALU.subtract)], r=[other, "late0"], w=[("B", oc, 1)])

        proj_tiles("win", lambda oc: (l, oc), 16, tiles, evac, after, head=(0 if pool_layer else 4))

        if pool_layer:
            for gi in range(4):
                wt, wr = P.wtile(("poolw", l, gi))
                for oc3 in range(3):
                    oc = gi * 3 + oc3
                    for (ti, lo, hi) in tiles:
                        n = hi - lo
                        bank = P.rot("p", PROJ)
                        if real:
                            mm_group(ps[bank][:, :n],
                                     [(wt[:, kc3 * 384 + oc3 * 128: kc3 * 384 + (oc3 + 1) * 128], regB[:, gi * 3 + kc3, lo:hi])
                                      for kc3 in range(3)],
                                     r=[wr] + [("B", gi * 3 + kc3, ti) for kc3 in range(3)], w=[PSB(bank)])
                            P.op("act", [A("activation", out=xn[:, oc, lo:hi], in_=ps[bank][:, :n], func=AF.Identity,
                                           scale=psc[:, l * 12 + oc: l * 12 + oc + 1])],
                                 r=["consts"], w=[PSB(bank), ("xn", oc, ti)])
        elif real:
            swa_attention(l - N_A)
        if real:
            mem_attention(tiles)

        def evac_out(oc, ti, lo, hi, bank):
            n = hi - lo
            P.op("dve", [A("tensor_tensor", out=h[:, oc, lo:hi], in0=ps[bank][:, :n], in1=h[:, oc, lo:hi], op=ALU.add)],
                 w=[PSB(bank), ("h", oc, ti)])
        proj_tiles("wout", lambda oc: (l, oc), 16, tiles, evac_out)

    def dump(k):
        if real and dbg is not None and k < dbg:
            P.dma(T["dbg"][k].rearrange("(c p) t -> p c t", p=128), h[:, :, :],
                  r=[("h", c, ti) for c in range(16) for ti in range(3)], w=[("dbgout", k)])
            P.final.append(("dbgout", k))

    k = 0
    for l in range(nlayers):
        tiles = TT_ALL if l < N_A else TT_MAIN
        ffn(l, 0, tiles)
        dump(k); k += 1
        mixer(l, tiles)
        dump(k); k += 1
        ffn(l, 1, tiles, extra=(rope_table_ops() if (l == N_A - 1 and nlayers > N_A) else ()))
        dump(k); k += 1
        if l == N_A - 1 and nlayers > N_A:
            make_kv()
    rmsnorm(14, TT_MAIN, h, hres, h, hres)
    if real:
        outT = T["outT"]
        for (ti, lo, hi) in TT_MAIN:
            for q in range(4):
                P.dma(outT[4 * q * 128:(4 * q + 4) * 128, lo - HALO:hi - HALO].rearrange("(c p) t -> p c t", p=128),
                      h[:, 4 * q:4 * q + 4, lo:hi], r=[("h", c, ti) for c in range(4 * q, 4 * q + 4)], w=[("out", q, ti)])
                P.final.append(("out", q, ti))


def plan_counts(plan):
    cnt = {}
    for p in plan:
        cnt[p[0]] = cnt.get(p[0], 0) + 1
    return cnt


def build(nlayers=DEPTH, dbg=None):
    P0 = Prog(None, None, None, None, None)
    emit_program(P0, {k: None for k in (
        "h xn regB late ps tmpA scr V mkT mv gains psc sinks nsinks invf ident rrot ones invc valid epst att "
        "B32 BI32 cosT sinT kT PU PA PB memn maskbf m3b").split()}, nlayers, dbg)
    plan = P0.plan

    nc = bass.Bass("TRN2", target_bir_lowering=False)
    dt = lambda name, shape, dtype=F32, kind="ExternalInput": nc.dram_tensor(name, shape, dtype, kind=kind).ap()
    NL, NP = nlayers, min(nlayers, N_A)
    wsrc = {
        "gate": dt("w_gate", [NL, 2, NFF, 128, 2048]),
        "up": dt("w_up", [NL, 2, NFF, 128, 2048]),
        "down": dt("w_down", [NL, 2, NGRP, 4, 128, 2048]),
        "win": dt("w_in", [NL, 16, 128, 2048]),
        "wout": dt("w_out", [NL, 16, 128, 2048]),
        "poolw": dt("w_pool", [NP, 4, 128, 2048]),
        "memk": dt("w_memk", [NL, 4, 128, 2048]),
        "memv": dt("w_memv", [NL, 4, 128, 2048]),
        "wk": dt("w_k", [3, 128, 2048]),
        "wv": dt("w_v", [2, 128, 2048]),
    }
    T = {}
    T["xT"] = dt("xT", [D, TB])
    T["cst"] = dt("cst", [128, C_N])
    T["ccst"] = dt("ccst", [128, CC_N])
    T["memT"] = dt("memT", [D, 256])
    T["pos"] = dt("pos", [128, NKEY], I32)
    T["outT"] = dt("outT", [D, TOK], F32, "ExternalOutput")
    if dbg is not None:
        T["dbg"] = dt("dbg", [dbg, D, TB], F32, "ExternalOutput")

    es = ExitStack()
    with es:
        E = es.enter_context
        sb = lambda name, shape, dtype: E(nc.sbuf_tensor(name, shape, dtype))
        T["h"] = sb("h", [128, 16, TB], F32)
        T["xn"] = sb("xn", [128, 16, TB], BF16)
        T["regB"] = sb("regB", [128, 16, TB], BF16)
        ring = sb("ring", [128, R, 2048], BF16)
        T["late"] = sb("late", [128, 4032], F32)
        T["tmpA"] = sb("tmpA", [128, 512], F32)
        T["V"] = sb("V", [128, 9, 192], BF16)
        T["mkT"] = sb("mkT", [128, 4, 256], BF16)
        T["mv"] = sb("mv", [128, 2, 512], BF16)
        T["gains"] = sb("gains", [128, 240], F32)
        T["psc"] = sb("psc", [128, 24], F32)
        T["sinks"] = sb("sinks", [128, 48], F32)
        T["nsinks"] = sb("nsinks", [128, 48], F32)
        T["invf"] = sb("invf", [128, 1], F32)
        T["maskbf"] = sb("maskbf", [128, 2, 256], BF16)
        T["ident"] = sb("ident", [128, 128], BF16)
        T["rrot"] = sb("rrot", [128, 128], F32)
        T["ones"] = sb("ones", [128, 128], BF16)
        T["invc"] = sb("invc", [128, 64], F32)
        T["valid"] = sb("valid", [128, 1], F32)
        T["epst"] = sb("epst", [128, 1], F32)
        T["att"] = {
            "stat": sb("a_stat", [128, 48], F32),
            "P": sb("a_P", [128, 3, 512], BF16),
            "D": sb("a_D", [128, 3, 2, 128], BF16),
            "PT": sb("a_PT", [128, 2, 512], BF16),
        }
        T["ps"] = [E(nc.psum_tensor("ps%d" % i, [128, 512], F32)) for i in range(8)]
        Bflat = T["regB"][:].rearrange("p a b -> p (a b)")
        T["B32"] = Bflat.bitcast(F32)
        T["BI32"] = Bflat.bitcast(I32)
        T["m3b"] = Bflat[:, 9472:13568].rearrange("p (c m) -> p c m", m=256)
        T["memn"] = Bflat[:, 13568:17664].rearrange("p (c m) -> p c m", m=256)
        lt = T["late"]
        T["PU"], T["PA"], T["PB"] = lt[:, 0:TB], lt[:, TB:2 * TB], lt[:, 2 * TB:3 * TB]
        T["cosT"], T["sinT"] = lt[:, 0:NKEY], lt[:, NKEY:2 * NKEY]
        T["kT"] = lt[:, 2 * NKEY:4032].bitcast(BF16).rearrange("p (k t) -> p k t", t=NKEY)
        T["scr"] = T["att"]["P"][:].rearrange("p a b -> p (a b)")
        P = Prog(nc, es, plan, ring, wsrc)
        emit_program(P, T, nlayers, dbg)
        assert P.wi == len(plan)
        block = E(nc.Block())
        P.replay(block)
    return nc, plan


def _tile_cols(W, ncol_chunks):
    K = W.shape[0] // 128
    return np.ascontiguousarray(W.reshape(K, 128, ncol_chunks, 128).transpose(2, 1, 0, 3)).reshape(ncol_chunks, 128, K * 128)


def prepare_shared(inp, NL=DEPTH):
    f = np.float32
    sh = {}
    NP = min(NL, N_A)
    g = np.empty((NL, 2, NFF, 128, 2048), f)
    u = np.empty((NL, 2, NFF, 128, 2048), f)
    dn = np.empty((NL, 2, NGRP, 4, 128, 2048), f)
    for l in range(NL):
        for j in range(2):
            g[l, j] = _tile_cols(inp["ffn_gate"][l, j], NFF)
            u[l, j] = _tile_cols(inp["ffn_up"][l, j], NFF)
            Wd = inp["ffn_down"][l, j]
            dn[l, j] = np.ascontiguousarray(Wd.reshape(NGRP, 4, 128, 4, 512).transpose(0, 3, 2, 1, 4)).reshape(NGRP, 4, 128, 2048)
    sh["w_gate"], sh["w_up"], sh["w_down"] = g, u, dn
    sh["w_in"] = np.stack([_tile_cols(inp["w_in"][l], 16) for l in range(NL)])
    sh["w_out"] = np.stack([_tile_cols(inp["w_out"][l], 16) for l in range(NL)])
    pw = np.zeros((NP, 4, 128, 2048), f)
    for l in range(NP):
        for gi in range(4):
            pw[l, gi, :, :1152] = inp["pool_w"][l, gi].reshape(3, 128, 384).transpose(1, 0, 2).reshape(128, 1152)
    sh["w_pool"] = pw
    sh["w_memk"] = np.stack([_tile_cols(inp["w_mem_kv"][l][:, :512], 4) for l in range(NL)])
    mvv = np.empty((NL, 4, 128, 2048), f)
    for l in range(NL):
        Wv = inp["w_mem_kv"][l][:, 512:]
        mvv[l] = Wv.reshape(4, 4, 128, 512).transpose(0, 2, 1, 3).reshape(4, 128, 2048)
    sh["w_memv"] = mvv
    wk = inp["w_kv"][:, :192]
    wkd = np.concatenate([np.concatenate([wk[:, hh * 64:(hh + 1) * 64]] * 2, axis=1) for hh in range(3)], axis=1)
    sh["w_k"] = _tile_cols(np.ascontiguousarray(wkd), 3)
    wv = np.zeros((2, 128, 2048), f)
    wv[:, :, :1536] = inp["w_kv"][:, 192:].reshape(2, 8, 128, 192).transpose(0, 2, 1, 3).reshape(2, 128, 1536)
    sh["w_v"] = wv
    sh["memT"] = np.ascontiguousarray(inp["mem"][0].T)
    cst = np.zeros((128, C_N), f)
    vecs = np.concatenate([inp["norms"].reshape(12, D), inp["kv_norm"][None], inp["mem_norm"][None], inp["final_norm"][None]], 0)
    cst[:, C_GAIN:C_GAIN + 240] = vecs.reshape(15, 16, 128).transpose(2, 0, 1).reshape(128, 240)
    cst[:, C_PSC:C_PSC + 24] = inp["pool_scale"].reshape(2, 12, 128).transpose(2, 0, 1).reshape(128, 24)
    cst[:, C_SINK:C_SINK + 48] = np.broadcast_to(inp["sinks"].reshape(1, 48), (128, 48))
    half = 32
    invfreq = (1.0 / (np.float32(10000.0) ** (np.arange(half, dtype=f) * f(2.0 / 64)))).astype(f)
    cst[:, C_INVF] = invfreq[np.arange(128) % 32]
    qi = np.arange(128)[:, None] + 128
    kj = np.arange(256)[None, :]
    rel = qi - kj
    band = (rel >= 0) & (rel < 128)
    cst[:, C_MASK:C_MASK + 256] = np.where(band, 0.0, MASKV)
    cst[:, C_ID:C_ID + 128] = np.eye(128, dtype=f)
    rr = np.zeros((128, 128), f)
    for m in range(128):
        b, i = (m // 64) * 64, m % 64
        if i < 32:
            rr[b + i + 32, m] = -1.0
        else:
            rr[b + i - 32, m] = 1.0
    cst[:, C_ROT:C_ROT + 128] = rr
    sh["cst"] = cst
    sh["_band"] = band
    return sh


def prepare_core(inp, sh, ci):
    f = np.float32
    s0 = ci * TOK
    x = inp["x"][0]
    xT = np.zeros((D, TB), f)
    lo = max(0, s0 - HALO)
    xT[:, TB - (s0 + TOK - lo):] = x[lo:s0 + TOK].T
    pos = np.zeros((NKEY,), np.int32)
    plo = max(0, s0 - 128)
    pos[NKEY - (s0 + TOK - plo):] = inp["positions"][0, plo:s0 + TOK]
    cc = np.zeros((128, CC_N), f)
    for gi, w in enumerate((2, 4, 8, 16)):
        if ci == 0:
            cc[:, CC_INVC + gi * 16: CC_INVC + (gi + 1) * 16] = 1.0 / np.minimum(np.arange(16) + 1.0, float(w))
        else:
            cc[:, CC_INVC + gi * 16: CC_INVC + (gi + 1) * 16] = 1.0 / w
    cc[:, CC_VALID] = 0.0 if ci == 0 else 1.0
    band = sh["_band"].copy()
    if ci == 0:
        band[:, :128] = False
    cc[:, CC_MASK0:CC_MASK0 + 256] = np.where(band, 0.0, MASKV)
    return {"xT": xT, "pos": np.ascontiguousarray(np.broadcast_to(pos[None, :], (128, NKEY))), "ccst": cc}


_CACHE = {}


def run(inp, nlayers=DEPTH, dbg=None, cores=NCORES, trace=False):
    key = (nlayers, dbg)
    if key not in _CACHE:
        _CACHE[key] = build(nlayers, dbg)
    nc, plan = _CACHE[key]
    sh = prepare_shared(inp, nlayers)
    shared = {k: v for k, v in sh.items() if not k.startswith("_")}
    in_maps = []
    for ci in range(cores):
        m = dict(shared)
        m.update(prepare_core(inp, sh, ci))
        in_maps.append(m)
    res = run_bass_kernel_spmd(nc, in_maps, core_ids=list(range(cores)), trace=trace)
    return res


def kernel(**inputs):
    inp = {k: np.asarray(v) for k, v in inputs.items()}
    res = run(inp)
    out = np.empty((1, SEQ, D), np.float32)
    for ci in range(NCORES):
        out[0, ci * TOK:(ci + 1) * TOK, :] = res.results[ci]["outT"].T
    return out
```

```python
import math
from contextlib import ExitStack
import numpy as np
import concourse.bass as bass
import concourse.mybir as mybir
from concourse.bass_utils import run_bass_kernel_spmd

F32, BF16, I32 = mybir.dt.float32, mybir.dt.bfloat16, mybir.dt.int32
AF = mybir.ActivationFunctionType
ALU = mybir.AluOpType
AX = mybir.AxisListType

NCORES = 8
D = 2048
DFF = 5632
SEQ = 8192
TOK = SEQ // NCORES
HALO = 160
TB = TOK + HALO
NKEY = TOK + 128
KOFF = TB - NKEY
NCH = 16
NFF = DFF // 128
GSZ = 4
NGRP = NFF // GSZ
MIXC = 12
R = 6
EPS = 1e-5
DEPTH = 4
N_A = 2
MASKV = -30000.0
USE_POW = False
TWO_PI = 2.0 * math.pi

C_GAIN = 0
C_PSC = 240
C_SINK = 264
C_INVF = 312
C_MASK = 313
C_ID = 569
C_ROT = 697
C_N = 825
CC_INVC = 0
CC_VALID = 64
CC_MASK0 = 65
CC_N = 321


class Prog:
    ENGS = ("pe", "act", "dve", "pool", "sp")

    def __init__(self, nc, es, plan, ring, wsrc):
        self.nc, self.es = nc, es
        self.real = plan is not None
        self.plan = plan if plan is not None else []
        self.ring, self.wsrc = ring, wsrc
        self.wi = 0
        self.pin = None
        self.wissued = 0
        self.streams = {e: [] for e in self.ENGS}
        self.sem, self.cnt = {}, {}
        self.lastw, self.rd = {}, {}
        self.seen = {e: {} for e in self.ENGS}
        self.rotc = {}
        self.nmisc = 0
        self.final = []

    def rot(self, key, banks):
        i = self.rotc.get(key, 0)
        self.rotc[key] = i + 1
        return banks[i % len(banks)]

    def getsem(self, name):
        if name not in self.sem:
            self.sem[name] = self.es.enter_context(self.nc.semaphore(name))
            self.cnt[name] = 0
        return self.sem[name]

    def op(self, eng, fn, r=(), w=(), dsem=None):
        if not self.real:
            return
        deps = {}

        def need(t):
            if t is not None and deps.get(t[0], 0) < t[1]:
                deps[t[0]] = t[1]
        w = list(w)
        if dsem is not None:
            w.append(("sem", dsem))
        for x in r:
            need(self.lastw.get(x))
        for x in w:
            need(self.lastw.get(x))
            for s, v in self.rd.get(x, {}).items():
                need((s, v))
        st = self.streams[eng]
        for s, v in deps.items():
            if eng == "pe" and s == "S_pe":
                continue
            if self.seen[eng].get(s, 0) >= v:
                continue
            self.seen[eng][s] = v
            st.append(("w", s, v))
        sname, inc = (dsem, 16) if dsem is not None else ("S_" + eng, 1)
        self.getsem(sname)
        self.cnt[sname] += inc
        tk = (sname, self.cnt[sname])
        st.append(("o", fn, sname, inc))
        for x in w:
            self.lastw[x] = tk
            self.rd[x] = {}
        for x in r:
            dd = self.rd.setdefault(x, {})
            if dd.get(sname, 0) < tk[1]:
                dd[sname] = tk[1]

    def dma(self, out, in_, r=(), w=(), eng="sp"):
        name = "m%d" % (self.nmisc % 16)
        self.nmisc += 1
        self.op(eng, [("dma_start", dict(out=out, in_=in_))], r=r, w=w, dsem=name)

    def wtile(self, src):
        i = self.wi
        self.wi += 1
        if not self.real:
            self.plan.append(src)
            return None, None
        assert self.plan[i] == src, (i, self.plan[i], src)
        limit = i + R - 3
        if self.pin is not None:
            limit = min(limit, self.pin + R - 1)
        while self.wissued < len(self.plan) and self.wissued <= limit:
            j = self.wissued
            self.wissued += 1
            sl = j % R
            kind, idx = self.plan[j][0], self.plan[j][1:]
            srcap = self.wsrc[kind][idx]
            dst = self.ring[:, sl, :]
            self.op("pool", [("dma_start", dict(out=dst, in_=srcap))], w=[("ring", sl)], dsem="slot%d" % sl)
        sl = i % R
        return self.ring[:, sl, :], ("ring", sl)

    def replay(self, block):
        def run(eng, name):
            for it in self.streams[name]:
                if it[0] == "w":
                    eng.wait_ge(self.sem[it[1]], it[2])
                else:
                    ins = None
                    for (mname, kw) in it[1]:
                        ins = getattr(eng, mname)(**kw)
                    ins.then_inc(self.sem[it[2]], it[3])
            if name == "sp":
                for res in self.final:
                    s, v = self.lastw[res]
                    eng.wait_ge(self.sem[s], v)
        block.sync(lambda e: run(e, "sp"))
        block.gpsimd(lambda e: run(e, "pool"))
        block.scalar(lambda e: run(e, "act"))
        block.vector(lambda e: run(e, "dve"))
        block.tensor(lambda e: run(e, "pe"))


def A(name, **kw):
    return (name, kw)


def emit_program(P, T, nlayers, dbg):
    real = P.real
    h, xn, regB, late, ps = T["h"], T["xn"], T["regB"], T["late"], T["ps"]
    tmpA, scr, V, mkT, mv = T["tmpA"], T["scr"], T["V"], T["mkT"], T["mv"]
    gains, psc, sinks, nsinks, invf = T["gains"], T["psc"], T["sinks"], T["nsinks"], T["invf"]
    maskbf, ident, rrot, ones, invc, valid, epst = T["maskbf"], T["ident"], T["rrot"], T["ones"], T["invc"], T["valid"], T["epst"]
    att = T["att"]
    B32, BI32 = T["B32"], T["BI32"]
    cosT, sinT, kT = T["cosT"], T["sinT"], T["kT"]
    PU, PA, PB = T["PU"], T["PA"], T["PB"]
    memn = T["memn"]

    TT_ALL = [(1, HALO, HALO + 512), (2, HALO + 512, TB), (0, 0, HALO)]
    TT_MAIN = TT_ALL[:2]
    ALLB = [("B", c, ti) for c in range(16) for ti in range(3)]
    LATE = ["late0", "late1", "late2"]
    PSB = lambda k: ("ps", k)
    PROJ = [0, 1, 2, 3, 4, 5]

    def mm_group(out_ap, pairs, r, w):
        n = len(pairs)
        P.op("pe", [A("matmul", out=out_ap, lhsT=a, rhs=b, start=(i == 0), stop=(i == n - 1)) for i, (a, b) in enumerate(pairs)],
             r=r, w=w)

    if real:
        xT, cst, ccst = T["xT"], T["cst"], T["ccst"]
        P.dma(B32[:, 0:C_N], cst, w=ALLB)
        P.dma(B32[:, C_N:C_N + CC_N], ccst, w=ALLB)
        for (ti, lo, hi) in TT_ALL:
            for q in range(4):
                P.dma(h[:, 4 * q:4 * q + 4, lo:hi], xT[4 * q * 128:(4 * q + 4) * 128, lo:hi].rearrange("(c p) t -> p c t", p=128),
                      w=[("h", c, ti) for c in range(4 * q, 4 * q + 4)], eng=("act" if q % 2 else "sp"))

        def cp(o, i):
            P.op("dve", [A("tensor_copy", out=o, in_=i)], r=ALLB, w=["consts"])
        cp(gains[:], B32[:, C_GAIN:C_GAIN + 240])
        cp(psc[:], B32[:, C_PSC:C_PSC + 24])
        cp(sinks[:], B32[:, C_SINK:C_SINK + 48])
        cp(invf[:], B32[:, C_INVF:C_INVF + 1])
        cp(maskbf[:, 0, :], B32[:, C_MASK:C_MASK + 256])
        cp(ident[:], B32[:, C_ID:C_ID + 128])
        cp(rrot[:], B32[:, C_ROT:C_ROT + 128])
        cp(invc[:], B32[:, C_N + CC_INVC:C_N + CC_INVC + 64])
        cp(valid[:], B32[:, C_N + CC_VALID:C_N + CC_VALID + 1])
        cp(maskbf[:, 1, :], B32[:, C_N + CC_MASK0:C_N + CC_MASK0 + 256])
        P.op("dve", [A("tensor_scalar", out=nsinks[:], in0=sinks[:], scalar1=-1.0, scalar2=None, op0=ALU.mult)], w=["consts"])
        P.op("dve", [A("memset", ap=ones[:], constant=1.0)], w=["consts"])
        P.op("dve", [A("memset", ap=epst[:], constant=EPS)], w=["consts"])

    def rmsnorm(gi, tiles, src, src_res, dst, dst_res, nchunk=16, inv_n=1.0 / D, dve_help=True):
        for idx, (ti, lo, hi) in enumerate(tiles):
            n = hi - lo
            bank = P.rot("n", [6, 7])
            if not real:
                continue
            pn = ps[bank][:, :n]
            for c in range(nchunk):
                sb = c % 2
                sqb = scr[:, sb * 512: sb * 512 + n]
                if dve_help and idx == 0 and sb == 1:
                    P.op("dve", [A("tensor_tensor", out=sqb, in0=src[:, c, lo:hi], in1=src[:, c, lo:hi], op=ALU.mult)],
                         r=src_res(c, ti), w=[("P", sb)])
                else:
                    P.op("act", [A("activation", out=sqb, in_=src[:, c, lo:hi], func=AF.Square)],
                         r=src_res(c, ti), w=[("P", sb)])
                P.op("pe", [A("matmul", out=pn, lhsT=ones[:], rhs=sqb, start=(c == 0), stop=(c == nchunk - 1))],
                     r=[("P", sb), "consts"], w=[PSB(bank)])
            if USE_POW:
                P.op("dve", [A("tensor_scalar", out=pn, in0=pn, scalar1=inv_n, scalar2=EPS, op0=ALU.mult, op1=ALU.add)],
                     w=[PSB(bank)])
                P.op("dve", [A("tensor_scalar", out=pn, in0=pn, scalar1=-0.5, scalar2=None, op0=ALU.pow)], w=[PSB(bank)])
            else:
                P.op("act", [A("activation", out=pn, in_=pn, func=AF.Sqrt, bias=epst[:, 0:1], scale=inv_n)],
                     r=["consts"], w=[PSB(bank)])
                P.op("dve", [A("reciprocal", out=pn, in_=pn)], w=[PSB(bank)])
            for c in range(nchunk):
                P.op("dve", [A("scalar_tensor_tensor", out=dst[:, c, lo:hi], in0=src[:, c, lo:hi],
                               scalar=gains[:, gi * 16 + c: gi * 16 + c + 1], in1=pn, op0=ALU.mult, op1=ALU.mult)],
                     r=src_res(c, ti) + ["consts"], w=[PSB(bank)] + dst_res(c, ti))

    hres = lambda c, ti: [("h", c, ti)]
    xres = lambda c, ti: [("xn", c, ti)]

    def proj_tiles(kind, idx_fn, noc, tiles, evac, after=None, head=0):
        def one(oc, wt, wr, tl):
            for (ti, lo, hi) in tl:
                n = hi - lo
                bank = P.rot("p", PROJ)
                if real:
                    mm_group(ps[bank][:, :n], [(wt[:, kc * 128:(kc + 1) * 128], xn[:, kc, lo:hi]) for kc in range(16)],
                             r=[wr] + [("xn", kc, ti) for kc in range(16)], w=[PSB(bank)])
                    evac(oc, ti, lo, hi, bank)
        if head:
            P.pin = P.wi
            held = []
            for oc in range(head):
                wt, wr = P.wtile((kind,) + idx_fn(oc))
                held.append((oc, wt, wr))
                one(oc, wt, wr, tiles[:1])
            for (oc, wt, wr) in held:
                one(oc, wt, wr, tiles[1:])
                if real and after is not None:
                    after(oc)
            P.pin = None
        for oc in range(head, noc):
            wt, wr = P.wtile((kind,) + idx_fn(oc))
            one(oc, wt, wr, tiles)
            if real and after is not None:
                after(oc)

    def ffn(l, j, tiles):
        def GU(g):
            buf = g % 2

            def one(c4, wg, rg, wu, ru, tl):
                for (ti, lo, hi) in tl:
                    n = hi - lo
                    bg = P.rot("g", [0, 1])
                    bu = P.rot("u", [2, 3])
                    if not real:
                        continue
                    xr = [("xn", kc, ti) for kc in range(16)]
                    mm_group(ps[bg][:, :n], [(wg[:, kc * 128:(kc + 1) * 128], xn[:, kc, lo:hi]) for kc in range(16)],
                             r=[rg] + xr, w=[PSB(bg)])
                    mm_group(ps[bu][:, :n], [(wu[:, kc * 128:(kc + 1) * 128], xn[:, kc, lo:hi]) for kc in range(16)],
                             r=[ru] + xr, w=[PSB(bu)])
                    P.op("act", [A("activation", out=tmpA[:, :n], in_=ps[bg][:, :n], func=AF.Silu)], w=[PSB(bg), "tmpA"])
                    hc = buf * GSZ + c4
                    P.op("dve", [A("tensor_tensor", out=regB[:, hc, lo:hi], in0=tmpA[:, :n], in1=ps[bu][:, :n], op=ALU.mult)],
                         r=["tmpA"], w=[PSB(bu), ("B", hc, ti)])

            c4s = list(range(GSZ))
            if g == 0:
                P.pin = P.wi
                held = []
                for c4 in c4s[:2]:
                    wg, rg = P.wtile(("gate", l, j, c4))
                    wu, ru = P.wtile(("up", l, j, c4))
                    held.append((c4, wg, rg, wu, ru))
                    one(c4, wg, rg, wu, ru, tiles[:1])
                for (c4, wg, rg, wu, ru) in held:
                    one(c4, wg, rg, wu, ru, tiles[1:])
                P.pin = None
                c4s = c4s[2:]
            for c4 in c4s:
                c = g * GSZ + c4
                wg, rg = P.wtile(("gate", l, j, c))
                wu, ru = P.wtile(("up", l, j, c))
                one(c4, wg, rg, wu, ru, tiles)

        def DN(g):
            buf = g % 2
            for dq in range(4):
                wd, rdn = P.wtile(("down", l, j, g, dq))
                for dcl in range(4):
                    dc = dq * 4 + dcl
                    for (ti, lo, hi) in tiles:
                        n = hi - lo
                        bd = P.rot("d", [4, 5, 6, 7])
                        if not real:
                            continue
                        mm_group(ps[bd][:, :n],
                                 [(wd[:, c4 * 512 + dcl * 128: c4 * 512 + (dcl + 1) * 128], regB[:, buf * GSZ + c4, lo:hi])
                                  for c4 in range(GSZ)],
                                 r=[rdn] + [("B", buf * GSZ + c4, ti) for c4 in range(GSZ)], w=[PSB(bd)])
                        P.op("dve", [A("scalar_tensor_tensor", out=h[:, dc, lo:hi], in0=ps[bd][:, :n], scalar=0.5,
                                       in1=h[:, dc, lo:hi], op0=ALU.mult, op1=ALU.add)], w=[PSB(bd), ("h", dc, ti)])

        if j == 0:
            mem_load(l)
        rmsnorm(l * 3 + (0 if j == 0 else 2), tiles, h, hres, xn, xres)
        GU(0)
        for g in range(NGRP):
            if g + 1 < NGRP:
                GU(g + 1)
            DN(g)
            if j == 0 and g == 1:
                mem_norm(l)
            if j == 0 and g == 3:
                mem_mm(l)

    def attention_jobs(jobs):
        nj = len(jobs)
        SBK = [0, 1, 2, 3]
        TBK = [4, 5]

        def sA(k):
            jb = jobs[k]
            jb["sbank"] = SBK[k % 4]
            jb["emit_S"](jb["sbank"])

        def sB1(k):
            jb = jobs[k]
            bank, nq, sb, sc = jb["sbank"], jb["nq"], k % 3, jb["scale"]
            st = att["stat"][0:nq, sb * 16:(sb + 1) * 16]
            s3 = ps[bank][0:nq, 0:512].rearrange("p (h j) -> p h j", h=2)
            P.op("dve", [A("tensor_reduce", out=st[:, 0:2], in_=s3, axis=AX.X, op=ALU.max)], r=[PSB(bank)], w=[("st", sb)])
            if jb["nsink"] is not None:
                P.op("dve", [A("scalar_tensor_tensor", out=st[:, 2:4], in0=st[:, 0:2], scalar=-sc, in1=jb["nsink"],
                               op0=ALU.mult, op1=ALU.min)], r=["consts"], w=[("st", sb)])
            else:
                P.op("dve", [A("tensor_scalar", out=st[:, 2:4], in0=st[:, 0:2], scalar1=-sc, scalar2=None, op0=ALU.mult)],
                     w=[("st", sb)])

        def sB2(k):
            jb = jobs[k]
            bank, nq, sb, sc = jb["sbank"], jb["nq"], k % 3, jb["scale"]
            st = att["stat"][0:nq, sb * 16:(sb + 1) * 16]
            Pm = att["P"][0:nq, sb, :]
            ins = [A("activation", out=Pm[:, hh * 256:(hh + 1) * 256], in_=ps[bank][0:nq, hh * 256:(hh + 1) * 256], func=AF.Exp,
                     bias=st[:, 2 + hh:3 + hh], scale=sc, accum_out=st[:, 4 + hh:5 + hh]) for hh in range(2)]
            if jb["sink"] is not None:
                ins += [A("activation", out=st[:, 6 + hh:7 + hh], in_=jb["sink"][:, hh:hh + 1], func=AF.Exp,
                          bias=st[:, 2 + hh:3 + hh], scale=1.0) for hh in range(2)]
            P.op("act", ins, r=[PSB(bank), ("st", sb), "consts"], w=[("P", sb), ("st2", sb)])

        def sB3(k):
            jb = jobs[k]
            nq, sb = jb["nq"], k % 3
            st = att["stat"][0:nq, sb * 16:(sb + 1) * 16]
            if jb["sink"] is not None:
                P.op("dve", [A("tensor_tensor", out=st[:, 8:10], in0=st[:, 4:6], in1=st[:, 6:8], op=ALU.add)],
                     r=[("st2", sb)], w=[("st3", sb)])
                P.op("dve", [A("reciprocal", out=st[:, 10:12], in_=st[:, 8:10])], w=[("st3", sb)])
            else:
                P.op("dve", [A("reciprocal", out=st[:, 10:12], in_=st[:, 4:6])], r=[("st2", sb)], w=[("st3", sb)])
            P.op("dve", [A("tensor_scalar", out=att["D"][0:nq, sb, hh, 0:nq], in0=ident[0:nq, 0:nq], scalar1=st[:, 10 + hh:11 + hh],
                           scalar2=None, op0=ALU.mult) for hh in range(2)], r=[("st3", sb), "consts"], w=[("D", sb)])

        def sC(k):
            jb = jobs[k]
            nq, sb, sp = jb["nq"], k % 3, k % 2
            bank = TBK[k % 2]
            Pm = att["P"][0:nq, sb, :]
            PT = att["PT"][:, sp, :]
            P.op("pe", [A("matmul", out=ps[bank][:, bi * 128: bi * 128 + nq], lhsT=Pm[:, bi * 128:(bi + 1) * 128],
                          rhs=att["D"][0:nq, sb, bi // 2, 0:nq], start=True, stop=True) for bi in range(4)],
                 r=[("P", sb), ("D", sb)], w=[PSB(bank)])
            if nq == 128:
                P.op("act", [A("activation", out=PT[:, 0:512], in_=ps[bank][:, 0:512], func=AF.Identity)],
                     r=[PSB(bank)], w=[("PT", sp)])
            else:
                P.op("act", [A("activation", out=PT[:, bi * 128: bi * 128 + nq], in_=ps[bank][:, bi * 128: bi * 128 + nq],
                               func=AF.Identity) for bi in range(4)], r=[PSB(bank)], w=[("PT", sp)])

        def sC2(k):
            jb = jobs[k]
            sp = k % 2
            jb["emit_PV"](att["PT"][:, sp, :], ("PT", sp))

        for t in range(nj + 5):
            if t < nj:
                sA(t)
            if 0 <= t - 1 < nj:
                sB1(t - 1)
            if 0 <= t - 2 < nj:
                sB2(t - 2)
            if 0 <= t - 3 < nj:
                sB3(t - 3)
            if 0 <= t - 4 < nj:
                sC(t - 4)
            if 0 <= t - 5 < nj:
                sC2(t - 5)

    MEMB = [("B", c, ti) for c in range(8, 16) for ti in range(3)]
    ALLB_ = MEMB

    def mem_load(l):
        if real:
            P.dma(T["m3b"], T["memT"].rearrange("(c p) m -> p c m", p=128), w=MEMB, eng="pool")

    def mem_norm(l):
        if real:
            rmsnorm(13, [(0, 0, 256)], T["m3b"], lambda c, ti: MEMB, memn, lambda c, ti: MEMB, dve_help=False)
        else:
            rmsnorm(13, [(0, 0, 256)], None, None, None, None)

    def mem_mm(l):
        for hm in range(4):
            wt, wr = P.wtile(("memk", l, hm))
            bank = P.rot("p", PROJ)
            if real:
                mm_group(ps[bank][:, 0:256], [(wt[:, kc * 128:(kc + 1) * 128], memn[:, kc, :]) for kc in range(16)],
                         r=[wr] + MEMB, w=[PSB(bank)])
                P.op("act", [A("activation", out=mkT[:, hm, :], in_=ps[bank][:, 0:256], func=AF.Identity)],
                     w=[PSB(bank), ("mkT",)])
        b0 = P.rot("p", PROJ)
        b1 = P.rot("p", PROJ)
        for t4 in range(4):
            wt, wr = P.wtile(("memv", l, t4))
            if real:
                for mc, bank in ((0, b0), (1, b1)):
                    P.op("pe", [A("matmul", out=ps[bank][:, :], lhsT=memn[:, t4 * 4 + kk, mc * 128:(mc + 1) * 128],
                                  rhs=wt[:, kk * 512:(kk + 1) * 512], start=(t4 * 4 + kk == 0), stop=(t4 * 4 + kk == 15))
                                for kk in range(4)], r=[wr] + MEMB, w=[PSB(bank)])
        if real:
            for mc, bank in ((0, b0), (1, b1)):
                P.op("act", [A("activation", out=mv[:, mc, :], in_=ps[bank][:, :], func=AF.Identity)],
                     w=[PSB(bank), ("mv",)])

    def mem_attention(tiles):
        jobs = []
        scale = 1.0 / math.sqrt(128.0)
        for (ti, lo, hi) in tiles:
            for q0 in range(lo, hi, 128):
                q1 = min(q0 + 128, hi)
                nq = q1 - q0
                for hm in (0, 2):
                    def emit_S(bank, hm=hm, q0=q0, q1=q1, nq=nq, ti=ti):
                        P.op("pe", [A("matmul", out=ps[bank][0:nq, hh * 256:(hh + 1) * 256], lhsT=regB[:, 12 + hm + hh, q0:q1],
                                      rhs=mkT[:, hm + hh, :], start=True, stop=True) for hh in range(2)],
                             r=[("B", 12 + hm, ti), ("B", 13 + hm, ti), ("mkT",)], w=[PSB(bank)])

                    def emit_PV(PT, ptres, hm=hm, q0=q0, q1=q1, nq=nq, ti=ti):
                        bank = P.rot("o", [6, 7])
                        P.op("pe", [A("matmul", out=ps[bank][:, hh * 128: hh * 128 + nq],
                                      lhsT=mv[:, half, (hm + hh) * 128:(hm + hh + 1) * 128],
                                      rhs=PT[:, (hh * 2 + half) * 128:(hh * 2 + half) * 128 + nq],
                                      start=(half == 0), stop=(half == 1)) for hh in range(2) for half in range(2)],
                             r=[ptres, ("mv",)], w=[PSB(bank)])
                        P.op("dve", [A("tensor_copy", out=xn[:, 12 + hm + hh, q0:q1], in_=ps[bank][:, hh * 128: hh * 128 + nq])
                                     for hh in range(2)], r=[PSB(bank)], w=[("xn", 12 + hm, ti), ("xn", 13 + hm, ti)])
                    jobs.append(dict(emit_S=emit_S, emit_PV=emit_PV, scale=scale, nsink=None, sink=None, nq=nq))
        attention_jobs(jobs)

    def rope_chunk(bank, n, kb0, out_ap, out_res):
        rb = P.rot("r", [6, 7])
        P.op("act", [A("activation", out=tmpA[:, :n], in_=ps[bank][:, :n], func=AF.Identity)], w=[PSB(bank), "tmpA"])
        P.op("pe", [A("matmul", out=ps[rb][:, :n], lhsT=rrot[:], rhs=tmpA[:, :n], start=True, stop=True)],
             r=["tmpA", "consts"], w=[PSB(rb)])
        P.op("dve", [A("tensor_tensor", out=ps[rb][:, :n], in0=ps[rb][:, :n], in1=sinT[:, kb0:kb0 + n], op=ALU.mult)],
             r=LATE, w=[PSB(rb)])
        P.op("dve", [A("tensor_tensor", out=tmpA[:, :n], in0=tmpA[:, :n], in1=cosT[:, kb0:kb0 + n], op=ALU.mult)],
             r=LATE, w=["tmpA"])
        P.op("dve", [A("tensor_tensor", out=out_ap, in0=tmpA[:, :n], in1=ps[rb][:, :n], op=ALU.add)],
             w=[PSB(rb), "tmpA"] + list(out_res))

    def make_kv():
        if real:
            posI = BI32[:, 0:NKEY]
            ang = B32[:, NKEY:2 * NKEY]
            tr = B32[:, 2 * NKEY:3 * NKEY]
            P.dma(posI, T["pos"], w=ALLB)
            P.op("dve", [A("tensor_copy", out=ang, in_=posI)], w=ALLB)
            P.op("dve", [A("tensor_scalar", out=ang, in0=ang, scalar1=invf[:, 0:1], scalar2=None, op0=ALU.mult)],
                 r=["consts"], w=ALLB)
            kI = BI32[:, 3 * NKEY:4 * NKEY]
            kF = B32[:, 4 * NKEY:5 * NKEY]
            C1 = 6.28125
            C2 = TWO_PI - 6.28125
            P.op("dve", [A("tensor_scalar", out=tr, in0=ang, scalar1=1.0 / TWO_PI, scalar2=None, op0=ALU.mult)], w=ALLB)
            P.op("dve", [A("tensor_copy", out=kI, in_=tr)], w=ALLB)
            P.op("dve", [A("tensor_copy", out=kF, in_=kI)], w=ALLB)
            P.op("dve", [A("scalar_tensor_tensor", out=tr, in0=kF, scalar=-C1, in1=ang, op0=ALU.mult, op1=ALU.add)], w=ALLB)
            P.op("dve", [A("scalar_tensor_tensor", out=tr, in0=kF, scalar=-C2, in1=tr, op0=ALU.mult, op1=ALU.add)], w=ALLB)
            P.op("dve", [A("tensor_scalar", out=ang, in0=tr, scalar1=3.1415925, scalar2=-3.1415925, op0=ALU.min, op1=ALU.max)],
                 w=ALLB)
            P.op("act", [A("activation", out=sinT, in_=ang, func=AF.Sin)], r=ALLB, w=LATE)
            P.op("dve", [A("tensor_scalar", out=tr, in0=tr, scalar1=math.pi / 2, scalar2=None, op0=ALU.add)], w=ALLB)
            P.op("dve", [A("tensor_scalar", out=kF, in0=tr, scalar1=math.pi, scalar2=-TWO_PI, op0=ALU.is_gt, op1=ALU.mult)], w=ALLB)
            P.op("dve", [A("tensor_tensor", out=tr, in0=tr, in1=kF, op=ALU.add)], w=ALLB)
            P.op("dve", [A("tensor_scalar", out=tr, in0=tr, scalar1=3.1415925, scalar2=-3.1415925, op0=ALU.min, op1=ALU.max)],
                 w=ALLB)
            P.op("act", [A("activation", out=cosT, in_=tr, func=AF.Sin)], r=ALLB, w=LATE)
        rmsnorm(12, TT_ALL, h, hres, xn, xres)
        for kvh in range(3):
            wt, wr = P.wtile(("wk", kvh))
            for (ti, lo, hi) in (TT_ALL[0], TT_ALL[1], (0, KOFF, HALO)):
                n = hi - lo
                bank = P.rot("p", PROJ)
                if real:
                    mm_group(ps[bank][:, :n], [(wt[:, kc * 128:(kc + 1) * 128], xn[:, kc, lo:hi]) for kc in range(16)],
                             r=[wr] + [("xn", kc, ti) for kc in range(16)], w=[PSB(bank)])
                    kb0 = lo - KOFF
                    rope_chunk(bank, n, kb0, kT[:, kvh, kb0:kb0 + n], LATE + ["kT"])
        w0, r0 = P.wtile(("wv", 0))
        w1, r1 = P.wtile(("wv", 1))
        if real:
            for blk in range(9):
                b0 = KOFF + blk * 128
                bank = P.rot("p", PROJ)
                tis = sorted({0 if b < HALO else (1 if b < HALO + 512 else 2) for b in (b0, b0 + 127)})
                mm_group(ps[bank][:, 0:192],
                         [(xn[:, kc, b0:b0 + 128], (w0 if kc < 8 else w1)[:, (kc % 8) * 192:(kc % 8 + 1) * 192]) for kc in range(16)],
                         r=[r0, r1] + [("xn", kc, ti) for kc in range(16) for ti in tis], w=[PSB(bank)])
                P.op("act", [A("activation", out=V[:, blk, :], in_=ps[bank][:, 0:192], func=AF.Identity)],
                     w=[PSB(bank), ("V",)])

    def swa_attention(lb):
        jobs = []
        for qb in range(8):
            q0 = HALO + qb * 128
            ti = 1 if qb < 4 else 2
            mi = 1 if qb == 0 else 0
            for hp in range(12):
                kvh = hp // 4

                def emit_S(bank, hp=hp, q0=q0, qb=qb, kvh=kvh, ti=ti, mi=mi):
                    ins = []
                    for hh in range(2):
                        p0 = hh * 64
                        ins.append(A("matmul", out=ps[bank][:, hh * 256:(hh + 1) * 256], lhsT=regB[p0:p0 + 64, hp, q0:q0 + 128],
                                     rhs=kT[p0:p0 + 64, kvh, qb * 128: qb * 128 + 256], start=True, stop=False))
                        ins.append(A("matmul", out=ps[bank][:, hh * 256:(hh + 1) * 256], lhsT=ident[:, :], rhs=maskbf[:, mi, :],
                                     start=False, stop=True))
                    P.op("pe", ins, r=[("B", hp, ti), "kT", "consts"] + LATE, w=[PSB(bank)])

                def emit_PV(PT, ptres, hp=hp, q0=q0, qb=qb, kvh=kvh, ti=ti):
                    bank = P.rot("o", [6, 7])
                    P.op("pe", [A("matmul", out=ps[bank][hh * 64:(hh + 1) * 64, 0:128], lhsT=V[:, qb + half, kvh * 64:(kvh + 1) * 64],
                                  rhs=PT[:, (hh * 2 + half) * 128:(hh * 2 + half + 1) * 128], start=(half == 0), stop=(half == 1))
                                for hh in range(2) for half in range(2)], r=[ptres, ("V",)], w=[PSB(bank)])
                    P.op("dve", [A("tensor_copy", out=xn[:, hp, q0:q0 + 128], in_=ps[bank][:, 0:128])],
                         r=[PSB(bank)], w=[("xn", hp, ti)])
                hd = 2 * hp
                jobs.append(dict(emit_S=emit_S, emit_PV=emit_PV, nq=128, scale=0.125,
                                 nsink=nsinks[:, lb * 24 + hd: lb * 24 + hd + 2],
                                 sink=sinks[:, lb * 24 + hd: lb * 24 + hd + 2]))
        attention_jobs(jobs)

    def mixer(l, tiles):
        pool_layer = l < N_A
        rmsnorm(l * 3 + 1, tiles, h, hres, xn, xres)
        t_lo, t_hi = min(t[1] for t in tiles), max(t[2] for t in tiles)

        def evac(oc, ti, lo, hi, bank):
            n = hi - lo
            if oc >= MIXC:
                P.op("act", [A("activation", out=regB[:, oc, lo:hi], in_=ps[bank][:, :n], func=AF.Identity)],
                     w=[PSB(bank), ("B", oc, ti)])
            elif pool_layer:
                if ti == 0:
                    P.op("act", [A("activation", out=PU[:, lo:hi], in_=ps[bank][:, :n], func=AF.Identity, scale=valid[:, 0:1])],
                         r=["consts"], w=[PSB(bank), "late0"])
                else:
                    P.op("act", [A("activation", out=PU[:, lo:hi], in_=ps[bank][:, :n], func=AF.Identity)],
                         w=[PSB(bank), "late0"])
            else:
                rope_chunk(bank, n, lo - KOFF, regB[:, oc, lo:hi], [("B", oc, ti)])

        def after(oc):
            if not pool_layer or oc >= MIXC:
                return
            gi = oc // 3
            wdw = (2, 4, 8, 16)[gi]
            bufs = {"late0": PU, "late1": PA, "late2": PB}
            cur = "late0"
            sh = 1
            while sh < wdw:
                nxt = "late1" if cur != "late1" else "late2"
                s_, d_ = bufs[cur], bufs[nxt]
                P.op("dve", [A("tensor_tensor", out=d_[:, t_lo + sh:t_hi], in0=s_[:, t_lo + sh:t_hi], in1=s_[:, t_lo:t_hi - sh],
                               op=ALU.add)], r=[cur], w=[nxt])
                P.op("dve", [A("tensor_copy", out=d_[:, t_lo:t_lo + sh], in_=s_[:, t_lo:t_lo + sh])], r=[cur], w=[nxt])
                cur = nxt
                sh *= 2
            S_ = bufs[cur]
            for (ti, lo, hi) in tiles:
                P.op("dve", [A("scalar_tensor_tensor", out=regB[:, oc, lo:hi], in0=S_[:, lo:hi], scalar=1.0 / wdw,
                               in1=PU[:, lo:hi], op0=ALU.mult, op1=ALU.subtract)], r=[cur, "late0"], w=[("B", oc, ti)])
            other = "late2" if cur != "late2" else "late1"
            Tm = bufs[other]
            P.op("dve", [A("tensor_tensor", out=Tm[:, 0:16], in0=S_[:, HALO:HALO + 16], in1=invc[:, gi * 16:(gi + 1) * 16],
                           op=ALU.mult)], r=[cur, "consts"], w=[other])
            P.op("dve", [A("tensor_tensor", out=regB[:, oc, HALO:HALO + 16], in0=Tm[:, 0:16], in1=PU[:, HALO:HALO + 16],
                           op=ALU.subtract)], r=[other, "late0"], w=[("B", oc, 1)])

        proj_tiles("win", lambda oc: (l, oc), 16, tiles, evac, after, head=(0 if pool_layer else 4))

        if pool_layer:
            for gi in range(4):
                wt, wr = P.wtile(("poolw", l, gi))
                for oc3 in range(3):
                    oc = gi * 3 + oc3
                    for (ti, lo, hi) in tiles:
                        n = hi - lo
                        bank = P.rot("p", PROJ)
                        if real:
                            mm_group(ps[bank][:, :n],
                                     [(wt[:, kc3 * 384 + oc3 * 128: kc3 * 384 + (oc3 + 1) * 128], regB[:, gi * 3 + kc3, lo:hi])
                                      for kc3 in range(3)],
                                     r=[wr] + [("B", gi * 3 + kc3, ti) for kc3 in range(3)], w=[PSB(bank)])
                            P.op("act", [A("activation", out=xn[:, oc, lo:hi], in_=ps[bank][:, :n], func=AF.Identity,
                                           scale=psc[:, l * 12 + oc: l * 12 + oc + 1])],
                                 r=["consts"], w=[PSB(bank), ("xn", oc, ti)])
        elif real:
            swa_attention(l - N_A)
        if real:
            mem_attention(tiles)

        def evac_out(oc, ti, lo, hi, bank):
            n = hi - lo
            P.op("dve", [A("tensor_tensor", out=h[:, oc, lo:hi], in0=ps[bank][:, :n], in1=h[:, oc, lo:hi], op=ALU.add)],
                 w=[PSB(bank), ("h", oc, ti)])
        proj_tiles("wout", lambda oc: (l, oc), 16, tiles, evac_out)

    def dump(k):
        if real and dbg is not None and k < dbg:
            P.dma(T["dbg"][k].rearrange("(c p) t -> p c t", p=128), h[:, :, :],
                  r=[("h", c, ti) for c in range(16) for ti in range(3)], w=[("dbgout", k)])
            P.final.append(("dbgout", k))

    k = 0
    for l in range(nlayers):
        tiles = TT_ALL if l < N_A else TT_MAIN
        ffn(l, 0, tiles)
        dump(k); k += 1
        mixer(l, tiles)
        dump(k); k += 1
        ffn(l, 1, tiles)
        dump(k); k += 1
        if l == N_A - 1 and nlayers > N_A:
            make_kv()
    rmsnorm(14, TT_MAIN, h, hres, h, hres)
    if real:
        outT = T["outT"]
        for (ti, lo, hi) in TT_MAIN:
            for q in range(4):
                P.dma(outT[4 * q * 128:(4 * q + 4) * 128, lo - HALO:hi - HALO].rearrange("(c p) t -> p c t", p=128),
                      h[:, 4 * q:4 * q + 4, lo:hi], r=[("h", c, ti) for c in range(4 * q, 4 * q + 4)], w=[("out", q, ti)],
                      eng=("act" if q % 2 else "sp"))
                P.final.append(("out", q, ti))


def plan_counts(plan):
    cnt = {}
    for p in plan:
        cnt[p[0]] = cnt.get(p[0], 0) + 1
    return cnt


def build(nlayers=DEPTH, dbg=None):
    P0 = Prog(None, None, None, None, None)
    emit_program(P0, {k: None for k in (
        "h xn regB late ps tmpA scr V mkT mv gains psc sinks nsinks invf ident rrot ones invc valid epst att "
        "B32 BI32 cosT sinT kT PU PA PB memn maskbf m3b").split()}, nlayers, dbg)
    plan = P0.plan

    nc = bass.Bass("TRN2", target_bir_lowering=False)
    dt = lambda name, shape, dtype=F32, kind="ExternalInput": nc.dram_tensor(name, shape, dtype, kind=kind).ap()
    NL, NP = nlayers, min(nlayers, N_A)
    wsrc = {
        "gate": dt("w_gate", [NL, 2, NFF, 128, 2048]),
        "up": dt("w_up", [NL, 2, NFF, 128, 2048]),
        "down": dt("w_down", [NL, 2, NGRP, 4, 128, 2048]),
        "win": dt("w_in", [NL, 16, 128, 2048]),
        "wout": dt("w_out", [NL, 16, 128, 2048]),
        "poolw": dt("w_pool", [NP, 4, 128, 2048]),
        "memk": dt("w_memk", [NL, 4, 128, 2048]),
        "memv": dt("w_memv", [NL, 4, 128, 2048]),
        "wk": dt("w_k", [3, 128, 2048]),
        "wv": dt("w_v", [2, 128, 2048]),
    }
    T = {}
    T["xT"] = dt("xT", [D, TB])
    T["cst"] = dt("cst", [128, C_N])
    T["ccst"] = dt("ccst", [128, CC_N])
    T["memT"] = dt("memT", [D, 256])
    T["pos"] = dt("pos", [128, NKEY], I32)
    T["outT"] = dt("outT", [D, TOK], F32, "ExternalOutput")
    if dbg is not None:
        T["dbg"] = dt("dbg", [dbg, D, TB], F32, "ExternalOutput")

    es = ExitStack()
    with es:
        E = es.enter_context
        sb = lambda name, shape, dtype: E(nc.sbuf_tensor(name, shape, dtype))
        T["h"] = sb("h", [128, 16, TB], F32)
        T["xn"] = sb("xn", [128, 16, TB], BF16)
        T["regB"] = sb("regB", [128, 16, TB], BF16)
        ring = sb("ring", [128, R, 2048], BF16)
        T["late"] = sb("late", [128, 4032], F32)
        T["tmpA"] = sb("tmpA", [128, 512], F32)
        T["V"] = sb("V", [128, 9, 192], BF16)
        T["mkT"] = sb("mkT", [128, 4, 256], BF16)
        T["mv"] = sb("mv", [128, 2, 512], BF16)
        T["gains"] = sb("gains", [128, 240], F32)
        T["psc"] = sb("psc", [128, 24], F32)
        T["sinks"] = sb("sinks", [128, 48], F32)
        T["nsinks"] = sb("nsinks", [128, 48], F32)
        T["invf"] = sb("invf", [128, 1], F32)
        T["maskbf"] = sb("maskbf", [128, 2, 256], BF16)
        T["ident"] = sb("ident", [128, 128], BF16)
        T["rrot"] = sb("rrot", [128, 128], F32)
        T["ones"] = sb("ones", [128, 128], BF16)
        T["invc"] = sb("invc", [128, 64], F32)
        T["valid"] = sb("valid", [128, 1], F32)
        T["epst"] = sb("epst", [128, 1], F32)
        T["att"] = {
            "stat": sb("a_stat", [128, 48], F32),
            "P": sb("a_P", [128, 3, 512], BF16),
            "D": sb("a_D", [128, 3, 2, 128], BF16),
            "PT": sb("a_PT", [128, 2, 512], BF16),
        }
        T["ps"] = [E(nc.psum_tensor("ps%d" % i, [128, 512], F32)) for i in range(8)]
        Bflat = T["regB"][:].rearrange("p a b -> p (a b)")
        T["B32"] = Bflat.bitcast(F32)
        T["BI32"] = Bflat.bitcast(I32)
        T["m3b"] = Bflat[:, 9472:13568].rearrange("p (c m) -> p c m", m=256)
        T["memn"] = Bflat[:, 13568:17664].rearrange("p (c m) -> p c m", m=256)
        lt = T["late"]
        T["PU"], T["PA"], T["PB"] = lt[:, 0:TB], lt[:, TB:2 * TB], lt[:, 2 * TB:3 * TB]
        T["cosT"], T["sinT"] = lt[:, 0:NKEY], lt[:, NKEY:2 * NKEY]
        T["kT"] = lt[:, 2 * NKEY:4032].bitcast(BF16).rearrange("p (k t) -> p k t", t=NKEY)
        T["scr"] = T["att"]["P"][:].rearrange("p a b -> p (a b)")
        P = Prog(nc, es, plan, ring, wsrc)
        emit_program(P, T, nlayers, dbg)
        assert P.wi == len(plan)
        block = E(nc.Block())
        P.replay(block)
    return nc, plan


def _tile_cols(W, ncol_chunks):
    K = W.shape[0] // 128
    return np.ascontiguousarray(W.reshape(K, 128, ncol_chunks, 128).transpose(2, 1, 0, 3)).reshape(ncol_chunks, 128, K * 128)


def prepare_shared(inp, NL=DEPTH):
    f = np.float32
    sh = {}
    NP = min(NL, N_A)
    g = np.empty((NL, 2, NFF, 128, 2048), f)
    u = np.empty((NL, 2, NFF, 128, 2048), f)
    dn = np.empty((NL, 2, NGRP, 4, 128, 2048), f)
    for l in range(NL):
        for j in range(2):
            g[l, j] = _tile_cols(inp["ffn_gate"][l, j], NFF)
            u[l, j] = _tile_cols(inp["ffn_up"][l, j], NFF)
            Wd = inp["ffn_down"][l, j]
            dn[l, j] = np.ascontiguousarray(Wd.reshape(NGRP, 4, 128, 4, 512).transpose(0, 3, 2, 1, 4)).reshape(NGRP, 4, 128, 2048)
    sh["w_gate"], sh["w_up"], sh["w_down"] = g, u, dn
    sh["w_in"] = np.stack([_tile_cols(inp["w_in"][l], 16) for l in range(NL)])
    sh["w_out"] = np.stack([_tile_cols(inp["w_out"][l], 16) for l in range(NL)])
    pw = np.zeros((NP, 4, 128, 2048), f)
    for l in range(NP):
        for gi in range(4):
            pw[l, gi, :, :1152] = inp["pool_w"][l, gi].reshape(3, 128, 384).transpose(1, 0, 2).reshape(128, 1152)
    sh["w_pool"] = pw
    sh["w_memk"] = np.stack([_tile_cols(inp["w_mem_kv"][l][:, :512], 4) for l in range(NL)])
    mvv = np.empty((NL, 4, 128, 2048), f)
    for l in range(NL):
        Wv = inp["w_mem_kv"][l][:, 512:]
        mvv[l] = Wv.reshape(4, 4, 128, 512).transpose(0, 2, 1, 3).reshape(4, 128, 2048)
    sh["w_memv"] = mvv
    wk = inp["w_kv"][:, :192]
    wkd = np.concatenate([np.concatenate([wk[:, hh * 64:(hh + 1) * 64]] * 2, axis=1) for hh in range(3)], axis=1)
    sh["w_k"] = _tile_cols(np.ascontiguousarray(wkd), 3)
    wv = np.zeros((2, 128, 2048), f)
    wv[:, :, :1536] = inp["w_kv"][:, 192:].reshape(2, 8, 128, 192).transpose(0, 2, 1, 3).reshape(2, 128, 1536)
    sh["w_v"] = wv
    sh["memT"] = np.ascontiguousarray(inp["mem"][0].T)
    cst = np.zeros((128, C_N), f)
    vecs = np.concatenate([inp["norms"].reshape(12, D), inp["kv_norm"][None], inp["mem_norm"][None], inp["final_norm"][None]], 0)
    cst[:, C_GAIN:C_GAIN + 240] = vecs.reshape(15, 16, 128).transpose(2, 0, 1).reshape(128, 240)
    cst[:, C_PSC:C_PSC + 24] = inp["pool_scale"].reshape(2, 12, 128).transpose(2, 0, 1).reshape(128, 24)
    cst[:, C_SINK:C_SINK + 48] = np.broadcast_to(inp["sinks"].reshape(1, 48), (128, 48))
    half = 32
    invfreq = (1.0 / (np.float32(10000.0) ** (np.arange(half, dtype=f) * f(2.0 / 64)))).astype(f)
    cst[:, C_INVF] = invfreq[np.arange(128) % 32]
    qi = np.arange(128)[:, None] + 128
    kj = np.arange(256)[None, :]
    rel = qi - kj
    band = (rel >= 0) & (rel < 128)
    cst[:, C_MASK:C_MASK + 256] = np.where(band, 0.0, MASKV)
    cst[:, C_ID:C_ID + 128] = np.eye(128, dtype=f)
    rr = np.zeros((128, 128), f)
    for m in range(128):
        b, i = (m // 64) * 64, m % 64
        if i < 32:
            rr[b + i + 32, m] = -1.0
        else:
            rr[b + i - 32, m] = 1.0
    cst[:, C_ROT:C_ROT + 128] = rr
    sh["cst"] = cst
    sh["_band"] = band
    return sh


def prepare_core(inp, sh, ci):
    f = np.float32
    s0 = ci * TOK
    x = inp["x"][0]
    xT = np.zeros((D, TB), f)
    lo = max(0, s0 - HALO)
    xT[:, TB - (s0 + TOK - lo):] = x[lo:s0 + TOK].T
    pos = np.zeros((NKEY,), np.int32)
    plo = max(0, s0 - 128)
    pos[NKEY - (s0 + TOK - plo):] = inp["positions"][0, plo:s0 + TOK]
    cc = np.zeros((128, CC_N), f)
    for gi, w in enumerate((2, 4, 8, 16)):
        if ci == 0:
            cc[:, CC_INVC + gi * 16: CC_INVC + (gi + 1) * 16] = 1.0 / np.minimum(np.arange(16) + 1.0, float(w))
        else:
            cc[:, CC_INVC + gi * 16: CC_INVC + (gi + 1) * 16] = 1.0 / w
    cc[:, CC_VALID] = 0.0 if ci == 0 else 1.0
    band = sh["_band"].copy()
    if ci == 0:
        band[:, :128] = False
    cc[:, CC_MASK0:CC_MASK0 + 256] = np.where(band, 0.0, MASKV)
    return {"xT": xT, "pos": np.ascontiguousarray(np.broadcast_to(pos[None, :], (128, NKEY))), "ccst": cc}


_CACHE = {}


def run(inp, nlayers=DEPTH, dbg=None, cores=NCORES, trace=False):
    key = (nlayers, dbg)
    if key not in _CACHE:
        _CACHE[key] = build(nlayers, dbg)
    nc, plan = _CACHE[key]
    sh = prepare_shared(inp, nlayers)
    shared = {k: v for k, v in sh.items() if not k.startswith("_")}
    in_maps = []
    for ci in range(cores):
        m = dict(shared)
        m.update(prepare_core(inp, sh, ci))
        in_maps.append(m)
    res = run_bass_kernel_spmd(nc, in_maps, core_ids=list(range(cores)), trace=trace)
    return res


def kernel(**inputs):
    inp = {k: np.asarray(v) for k, v in inputs.items()}
    res = run(inp)
    out = np.empty((1, SEQ, D), np.float32)
    for ci in range(NCORES):
        out[0, ci * TOK:(ci + 1) * TOK, :] = res.results[ci]["outT"].T
    return out
```

```python
import math
from contextlib import ExitStack
import numpy as np
import concourse.bass as bass
import concourse.mybir as mybir
from concourse.bass_utils import run_bass_kernel_spmd

F32, BF16, I32 = mybir.dt.float32, mybir.dt.bfloat16, mybir.dt.int32
AF = mybir.ActivationFunctionType
ALU = mybir.AluOpType
AX = mybir.AxisListType

NCORES = 8
D = 2048
DFF = 5632
SEQ = 8192
TOK = SEQ // NCORES
HALO = 160
TB = TOK + HALO
NKEY = TOK + 128
KOFF = TB - NKEY
NCH = 16
NFF = DFF // 128
GSZ = 4
NGRP = NFF // GSZ
MIXC = 12
R = 6
EPS = 1e-5
DEPTH = 4
N_A = 2
MASKV = -30000.0
USE_POW = False
TWO_PI = 2.0 * math.pi

C_GAIN = 0
C_PSC = 240
C_SINK = 264
C_INVF = 312
C_MASK = 313
C_ID = 569
C_ROT = 697
C_N = 825
CC_INVC = 0
CC_VALID = 64
CC_MASK0 = 65
CC_N = 321


class Prog:
    ENGS = ("pe", "act", "dve", "pool", "sp")

    def __init__(self, nc, es, plan, ring, wsrc):
        self.nc, self.es = nc, es
        self.real = plan is not None
        self.plan = plan if plan is not None else []
        self.ring, self.wsrc = ring, wsrc
        self.wi = 0
        self.pin = None
        self.wissued = 0
        self.streams = {e: [] for e in self.ENGS}
        self.sem, self.cnt = {}, {}
        self.lastw, self.rd = {}, {}
        self.seen = {e: {} for e in self.ENGS}
        self.rotc = {}
        self.nmisc = 0
        self.final = []

    def rot(self, key, banks):
        i = self.rotc.get(key, 0)
        self.rotc[key] = i + 1
        return banks[i % len(banks)]

    def getsem(self, name):
        if name not in self.sem:
            self.sem[name] = self.es.enter_context(self.nc.semaphore(name))
            self.cnt[name] = 0
        return self.sem[name]

    def op(self, eng, fn, r=(), w=(), dsem=None):
        if not self.real:
            return
        deps = {}

        def need(t):
            if t is not None and deps.get(t[0], 0) < t[1]:
                deps[t[0]] = t[1]
        w = list(w)
        if dsem is not None:
            w.append(("sem", dsem))
        for x in r:
            need(self.lastw.get(x))
        for x in w:
            need(self.lastw.get(x))
            for s, v in self.rd.get(x, {}).items():
                need((s, v))
        st = self.streams[eng]
        for s, v in deps.items():
            if eng == "pe" and s == "S_pe":
                continue
            if self.seen[eng].get(s, 0) >= v:
                continue
            self.seen[eng][s] = v
            st.append(("w", s, v))
        sname, inc = (dsem, 16) if dsem is not None else ("S_" + eng, 1)
        self.getsem(sname)
        self.cnt[sname] += inc
        tk = (sname, self.cnt[sname])
        st.append(("o", fn, sname, inc))
        for x in w:
            self.lastw[x] = tk
            self.rd[x] = {}
        for x in r:
            dd = self.rd.setdefault(x, {})
            if dd.get(sname, 0) < tk[1]:
                dd[sname] = tk[1]

    def dma(self, out, in_, r=(), w=(), eng="sp"):
        name = "m%d" % (self.nmisc % 16)
        self.nmisc += 1
        self.op(eng, [("dma_start", dict(out=out, in_=in_))], r=r, w=w, dsem=name)

    def wtile(self, src):
        i = self.wi
        self.wi += 1
        if not self.real:
            self.plan.append(src)
            return None, None
        assert self.plan[i] == src, (i, self.plan[i], src)
        limit = i + R - 3
        if self.pin is not None:
            limit = min(limit, self.pin + R - 1)
        while self.wissued < len(self.plan) and self.wissued <= limit:
            j = self.wissued
            self.wissued += 1
            sl = j % R
            kind, idx = self.plan[j][0], self.plan[j][1:]
            srcap = self.wsrc[kind][idx]
            dst = self.ring[:, sl, :]
            self.op("pool", [("dma_start", dict(out=dst, in_=srcap))], w=[("ring", sl)], dsem="slot%d" % sl)
        sl = i % R
        return self.ring[:, sl, :], ("ring", sl)

    def replay(self, block):
        def run(eng, name):
            for it in self.streams[name]:
                if it[0] == "w":
                    eng.wait_ge(self.sem[it[1]], it[2])
                else:
                    ins = None
                    for (mname, kw) in it[1]:
                        ins = getattr(eng, mname)(**kw)
                    ins.then_inc(self.sem[it[2]], it[3])
            if name == "sp":
                for res in self.final:
                    s, v = self.lastw[res]
                    eng.wait_ge(self.sem[s], v)
        block.sync(lambda e: run(e, "sp"))
        block.gpsimd(lambda e: run(e, "pool"))
        block.scalar(lambda e: run(e, "act"))
        block.vector(lambda e: run(e, "dve"))
        block.tensor(lambda e: run(e, "pe"))


def A(name, **kw):
    return (name, kw)


def emit_program(P, T, nlayers, dbg):
    real = P.real
    h, xn, regB, late, ps = T["h"], T["xn"], T["regB"], T["late"], T["ps"]
    tmpA, scr, V, mkT, mv = T["tmpA"], T["scr"], T["V"], T["mkT"], T["mv"]
    gains, psc, sinks, nsinks, invf = T["gains"], T["psc"], T["sinks"], T["nsinks"], T["invf"]
    maskbf, ident, rrot, ones, invc, valid, epst = T["maskbf"], T["ident"], T["rrot"], T["ones"], T["invc"], T["valid"], T["epst"]
    att = T["att"]
    B32, BI32 = T["B32"], T["BI32"]
    cosT, sinT, kT = T["cosT"], T["sinT"], T["kT"]
    PU, PA, PB = T["PU"], T["PA"], T["PB"]
    memn = T["memn"]

    TT_ALL = [(1, HALO, HALO + 512), (2, HALO + 512, TB), (0, 0, HALO)]
    TT_MAIN = TT_ALL[:2]
    ALLB = [("B", c, ti) for c in range(16) for ti in range(3)]
    LATE = ["late0", "late1", "late2"]
    PSB = lambda k: ("ps", k)
    PROJ = [0, 1, 2, 3, 4, 5]

    def mm_group(out_ap, pairs, r, w):
        n = len(pairs)
        P.op("pe", [A("matmul", out=out_ap, lhsT=a, rhs=b, start=(i == 0), stop=(i == n - 1)) for i, (a, b) in enumerate(pairs)],
             r=r, w=w)

    if real:
        xT, cst, ccst = T["xT"], T["cst"], T["ccst"]
        P.dma(B32[:, 0:C_N], cst, w=ALLB)
        P.dma(B32[:, C_N:C_N + CC_N], ccst, w=ALLB)
        for (ti, lo, hi) in TT_ALL:
            for q in range(4):
                P.dma(h[:, 4 * q:4 * q + 4, lo:hi], xT[4 * q * 128:(4 * q + 4) * 128, lo:hi].rearrange("(c p) t -> p c t", p=128),
                      w=[("h", c, ti) for c in range(4 * q, 4 * q + 4)], eng=("act" if q % 2 else "sp"))

        def cp(o, i):
            P.op("dve", [A("tensor_copy", out=o, in_=i)], r=ALLB, w=["consts"])
        cp(gains[:], B32[:, C_GAIN:C_GAIN + 240])
        cp(psc[:], B32[:, C_PSC:C_PSC + 24])
        cp(sinks[:], B32[:, C_SINK:C_SINK + 48])
        cp(invf[:], B32[:, C_INVF:C_INVF + 1])
        cp(maskbf[:, 0, :], B32[:, C_MASK:C_MASK + 256])
        cp(ident[:], B32[:, C_ID:C_ID + 128])
        cp(rrot[:], B32[:, C_ROT:C_ROT + 128])
        cp(invc[:], B32[:, C_N + CC_INVC:C_N + CC_INVC + 64])
        cp(valid[:], B32[:, C_N + CC_VALID:C_N + CC_VALID + 1])
        cp(maskbf[:, 1, :], B32[:, C_N + CC_MASK0:C_N + CC_MASK0 + 256])
        P.op("dve", [A("tensor_scalar", out=nsinks[:], in0=sinks[:], scalar1=-1.0, scalar2=None, op0=ALU.mult)], w=["consts"])
        P.op("dve", [A("memset", ap=ones[:], constant=1.0)], w=["consts"])
        P.op("dve", [A("memset", ap=epst[:], constant=EPS)], w=["consts"])

    def rmsnorm(gi, tiles, src, src_res, dst, dst_res, nchunk=16, inv_n=1.0 / D, dve_help=True):
        for idx, (ti, lo, hi) in enumerate(tiles):
            n = hi - lo
            bank = P.rot("n", [6, 7])
            if not real:
                continue
            pn = ps[bank][:, :n]
            for c in range(nchunk):
                sb = c % 2
                sqb = scr[:, sb * 512: sb * 512 + n]
                if dve_help and idx == 0 and sb == 1:
                    P.op("dve", [A("tensor_tensor", out=sqb, in0=src[:, c, lo:hi], in1=src[:, c, lo:hi], op=ALU.mult)],
                         r=src_res(c, ti), w=[("P", sb)])
                else:
                    P.op("act", [A("activation", out=sqb, in_=src[:, c, lo:hi], func=AF.Square)],
                         r=src_res(c, ti), w=[("P", sb)])
                P.op("pe", [A("matmul", out=pn, lhsT=ones[:], rhs=sqb, start=(c == 0), stop=(c == nchunk - 1))],
                     r=[("P", sb), "consts"], w=[PSB(bank)])
            if USE_POW:
                P.op("dve", [A("tensor_scalar", out=pn, in0=pn, scalar1=inv_n, scalar2=EPS, op0=ALU.mult, op1=ALU.add)],
                     w=[PSB(bank)])
                P.op("dve", [A("tensor_scalar", out=pn, in0=pn, scalar1=-0.5, scalar2=None, op0=ALU.pow)], w=[PSB(bank)])
            else:
                P.op("act", [A("activation", out=pn, in_=pn, func=AF.Sqrt, bias=epst[:, 0:1], scale=inv_n)],
                     r=["consts"], w=[PSB(bank)])
                P.op("dve", [A("reciprocal", out=pn, in_=pn)], w=[PSB(bank)])
            for c in range(nchunk):
                P.op("dve", [A("scalar_tensor_tensor", out=dst[:, c, lo:hi], in0=src[:, c, lo:hi],
                               scalar=gains[:, gi * 16 + c: gi * 16 + c + 1], in1=pn, op0=ALU.mult, op1=ALU.mult)],
                     r=src_res(c, ti) + ["consts"], w=[PSB(bank)] + dst_res(c, ti))

    hres = lambda c, ti: [("h", c, ti)]
    xres = lambda c, ti: [("xn", c, ti)]

    def proj_tiles(kind, idx_fn, noc, tiles, evac, after=None, head=0):
        def one(oc, wt, wr, tl):
            for (ti, lo, hi) in tl:
                n = hi - lo
                bank = P.rot("p", PROJ)
                if real:
                    mm_group(ps[bank][:, :n], [(wt[:, kc * 128:(kc + 1) * 128], xn[:, kc, lo:hi]) for kc in range(16)],
                             r=[wr] + [("xn", kc, ti) for kc in range(16)], w=[PSB(bank)])
                    evac(oc, ti, lo, hi, bank)
        if head:
            P.pin = P.wi
            held = []
            for oc in range(head):
                wt, wr = P.wtile((kind,) + idx_fn(oc))
                held.append((oc, wt, wr))
                one(oc, wt, wr, tiles[:1])
            for (oc, wt, wr) in held:
                one(oc, wt, wr, tiles[1:])
                if real and after is not None:
                    after(oc)
            P.pin = None
        for oc in range(head, noc):
            wt, wr = P.wtile((kind,) + idx_fn(oc))
            one(oc, wt, wr, tiles)
            if real and after is not None:
                after(oc)

    def ffn(l, j, tiles, extra=()):
        extra = list(extra)
        def GU(g):
            buf = g % 2

            def one(c4, wg, rg, wu, ru, tl):
                for (ti, lo, hi) in tl:
                    n = hi - lo
                    bg = P.rot("g", [0, 1])
                    bu = P.rot("u", [2, 3])
                    if not real:
                        continue
                    xr = [("xn", kc, ti) for kc in range(16)]
                    mm_group(ps[bg][:, :n], [(wg[:, kc * 128:(kc + 1) * 128], xn[:, kc, lo:hi]) for kc in range(16)],
                             r=[rg] + xr, w=[PSB(bg)])
                    mm_group(ps[bu][:, :n], [(wu[:, kc * 128:(kc + 1) * 128], xn[:, kc, lo:hi]) for kc in range(16)],
                             r=[ru] + xr, w=[PSB(bu)])
                    P.op("act", [A("activation", out=tmpA[:, :n], in_=ps[bg][:, :n], func=AF.Silu)], w=[PSB(bg), "tmpA"])
                    hc = buf * GSZ + c4
                    P.op("dve", [A("tensor_tensor", out=regB[:, hc, lo:hi], in0=tmpA[:, :n], in1=ps[bu][:, :n], op=ALU.mult)],
                         r=["tmpA"], w=[PSB(bu), ("B", hc, ti)])

            c4s = list(range(GSZ))
            if g == 0:
                P.pin = P.wi
                held = []
                for c4 in c4s[:2]:
                    wg, rg = P.wtile(("gate", l, j, c4))
                    wu, ru = P.wtile(("up", l, j, c4))
                    held.append((c4, wg, rg, wu, ru))
                    one(c4, wg, rg, wu, ru, tiles[:1])
                for (c4, wg, rg, wu, ru) in held:
                    one(c4, wg, rg, wu, ru, tiles[1:])
                P.pin = None
                c4s = c4s[2:]
            for c4 in c4s:
                c = g * GSZ + c4
                wg, rg = P.wtile(("gate", l, j, c))
                wu, ru = P.wtile(("up", l, j, c))
                one(c4, wg, rg, wu, ru, tiles)
                if extra and g >= 1:
                    extra.pop(0)()

        def DN(g):
            buf = g % 2
            for dq in range(4):
                wd, rdn = P.wtile(("down", l, j, g, dq))
                for dcl in range(4):
                    dc = dq * 4 + dcl
                    for (ti, lo, hi) in tiles:
                        n = hi - lo
                        bd = P.rot("d", [4, 5, 6, 7])
                        if not real:
                            continue
                        mm_group(ps[bd][:, :n],
                                 [(wd[:, c4 * 512 + dcl * 128: c4 * 512 + (dcl + 1) * 128], regB[:, buf * GSZ + c4, lo:hi])
                                  for c4 in range(GSZ)],
                                 r=[rdn] + [("B", buf * GSZ + c4, ti) for c4 in range(GSZ)], w=[PSB(bd)])
                        P.op("dve", [A("scalar_tensor_tensor", out=h[:, dc, lo:hi], in0=ps[bd][:, :n], scalar=0.5,
                                       in1=h[:, dc, lo:hi], op0=ALU.mult, op1=ALU.add)], w=[PSB(bd), ("h", dc, ti)])

        if j == 0:
            mem_load(l)
        rmsnorm(l * 3 + (0 if j == 0 else 2), tiles, h, hres, xn, xres)
        GU(0)
        for g in range(NGRP):
            if g + 1 < NGRP:
                GU(g + 1)
            DN(g)
            if j == 0 and g == 1:
                mem_norm(l)
            if j == 0 and g == 3:
                mem_mm(l)
        while extra:
            extra.pop(0)()

    def attention_jobs(jobs):
        nj = len(jobs)
        SBK = [0, 1, 2, 3]
        TBK = [4, 5]

        def sA(k):
            jb = jobs[k]
            jb["sbank"] = SBK[k % 4]
            jb["emit_S"](jb["sbank"])

        def sB1(k):
            jb = jobs[k]
            bank, nq, sb, sc = jb["sbank"], jb["nq"], k % 3, jb["scale"]
            st = att["stat"][0:nq, sb * 16:(sb + 1) * 16]
            s3 = ps[bank][0:nq, 0:512].rearrange("p (h j) -> p h j", h=2)
            P.op("dve", [A("tensor_reduce", out=st[:, 0:2], in_=s3, axis=AX.X, op=ALU.max)], r=[PSB(bank)], w=[("st", sb)])
            if jb["nsink"] is not None:
                P.op("dve", [A("scalar_tensor_tensor", out=st[:, 2:4], in0=st[:, 0:2], scalar=-sc, in1=jb["nsink"],
                               op0=ALU.mult, op1=ALU.min)], r=["consts"], w=[("st", sb)])
            else:
                P.op("dve", [A("tensor_scalar", out=st[:, 2:4], in0=st[:, 0:2], scalar1=-sc, scalar2=None, op0=ALU.mult)],
                     w=[("st", sb)])

        def sB2(k):
            jb = jobs[k]
            bank, nq, sb, sc = jb["sbank"], jb["nq"], k % 3, jb["scale"]
            st = att["stat"][0:nq, sb * 16:(sb + 1) * 16]
            Pm = att["P"][0:nq, sb, :]
            ins = [A("activation", out=Pm[:, hh * 256:(hh + 1) * 256], in_=ps[bank][0:nq, hh * 256:(hh + 1) * 256], func=AF.Exp,
                     bias=st[:, 2 + hh:3 + hh], scale=sc, accum_out=st[:, 4 + hh:5 + hh]) for hh in range(2)]
            if jb["sink"] is not None:
                ins += [A("activation", out=st[:, 6 + hh:7 + hh], in_=jb["sink"][:, hh:hh + 1], func=AF.Exp,
                          bias=st[:, 2 + hh:3 + hh], scale=1.0) for hh in range(2)]
            P.op("act", ins, r=[PSB(bank), ("st", sb), "consts"], w=[("P", sb), ("st2", sb)])

        def sB3(k):
            jb = jobs[k]
            nq, sb = jb["nq"], k % 3
            st = att["stat"][0:nq, sb * 16:(sb + 1) * 16]
            if jb["sink"] is not None:
                P.op("dve", [A("tensor_tensor", out=st[:, 8:10], in0=st[:, 4:6], in1=st[:, 6:8], op=ALU.add)],
                     r=[("st2", sb)], w=[("st3", sb)])
                P.op("dve", [A("reciprocal", out=st[:, 10:12], in_=st[:, 8:10])], w=[("st3", sb)])
            else:
                P.op("dve", [A("reciprocal", out=st[:, 10:12], in_=st[:, 4:6])], r=[("st2", sb)], w=[("st3", sb)])
            P.op("dve", [A("tensor_scalar", out=att["D"][0:nq, sb, hh, 0:nq], in0=ident[0:nq, 0:nq], scalar1=st[:, 10 + hh:11 + hh],
                           scalar2=None, op0=ALU.mult) for hh in range(2)], r=[("st3", sb), "consts"], w=[("D", sb)])

        def sC(k):
            jb = jobs[k]
            nq, sb, sp = jb["nq"], k % 3, k % 2
            bank = TBK[k % 2]
            Pm = att["P"][0:nq, sb, :]
            PT = att["PT"][:, sp, :]
            P.op("pe", [A("matmul", out=ps[bank][:, bi * 128: bi * 128 + nq], lhsT=Pm[:, bi * 128:(bi + 1) * 128],
                          rhs=att["D"][0:nq, sb, bi // 2, 0:nq], start=True, stop=True) for bi in range(4)],
                 r=[("P", sb), ("D", sb)], w=[PSB(bank)])
            if nq == 128:
                P.op("act", [A("activation", out=PT[:, 0:512], in_=ps[bank][:, 0:512], func=AF.Identity)],
                     r=[PSB(bank)], w=[("PT", sp)])
            else:
                P.op("act", [A("activation", out=PT[:, bi * 128: bi * 128 + nq], in_=ps[bank][:, bi * 128: bi * 128 + nq],
                               func=AF.Identity) for bi in range(4)], r=[PSB(bank)], w=[("PT", sp)])

        def sC2(k):
            jb = jobs[k]
            sp = k % 2
            jb["emit_PV"](att["PT"][:, sp, :], ("PT", sp))

        for t in range(nj + 5):
            if t < nj:
                sA(t)
            if 0 <= t - 1 < nj:
                sB1(t - 1)
            if 0 <= t - 2 < nj:
                sB2(t - 2)
            if 0 <= t - 3 < nj:
                sB3(t - 3)
            if 0 <= t - 4 < nj:
                sC(t - 4)
            if 0 <= t - 5 < nj:
                sC2(t - 5)

    MEMB = [("B", c, ti) for c in range(8, 16) for ti in range(3)]
    ALLB_ = MEMB

    def mem_load(l):
        if real:
            P.dma(T["m3b"], T["memT"].rearrange("(c p) m -> p c m", p=128), w=MEMB, eng="pool")

    def mem_norm(l):
        if real:
            rmsnorm(13, [(0, 0, 256)], T["m3b"], lambda c, ti: MEMB, memn, lambda c, ti: MEMB, dve_help=False)
        else:
            rmsnorm(13, [(0, 0, 256)], None, None, None, None)

    def mem_mm(l):
        for hm in range(4):
            wt, wr = P.wtile(("memk", l, hm))
            bank = P.rot("p", PROJ)
            if real:
                mm_group(ps[bank][:, 0:256], [(wt[:, kc * 128:(kc + 1) * 128], memn[:, kc, :]) for kc in range(16)],
                         r=[wr] + MEMB, w=[PSB(bank)])
                P.op("act", [A("activation", out=mkT[:, hm, :], in_=ps[bank][:, 0:256], func=AF.Identity)],
                     w=[PSB(bank), ("mkT",)])
        b0 = P.rot("p", PROJ)
        b1 = P.rot("p", PROJ)
        for t4 in range(4):
            wt, wr = P.wtile(("memv", l, t4))
            if real:
                for mc, bank in ((0, b0), (1, b1)):
                    P.op("pe", [A("matmul", out=ps[bank][:, :], lhsT=memn[:, t4 * 4 + kk, mc * 128:(mc + 1) * 128],
                                  rhs=wt[:, kk * 512:(kk + 1) * 512], start=(t4 * 4 + kk == 0), stop=(t4 * 4 + kk == 15))
                                for kk in range(4)], r=[wr] + MEMB, w=[PSB(bank)])
        if real:
            for mc, bank in ((0, b0), (1, b1)):
                P.op("act", [A("activation", out=mv[:, mc, :], in_=ps[bank][:, :], func=AF.Identity)],
                     w=[PSB(bank), ("mv",)])

    def mem_attention(tiles):
        jobs = []
        scale = 1.0 / math.sqrt(128.0)
        for (ti, lo, hi) in tiles:
            for q0 in range(lo, hi, 128):
                q1 = min(q0 + 128, hi)
                nq = q1 - q0
                for hm in (0, 2):
                    def emit_S(bank, hm=hm, q0=q0, q1=q1, nq=nq, ti=ti):
                        P.op("pe", [A("matmul", out=ps[bank][0:nq, hh * 256:(hh + 1) * 256], lhsT=regB[:, 12 + hm + hh, q0:q1],
                                      rhs=mkT[:, hm + hh, :], start=True, stop=True) for hh in range(2)],
                             r=[("B", 12 + hm, ti), ("B", 13 + hm, ti), ("mkT",)], w=[PSB(bank)])

                    def emit_PV(PT, ptres, hm=hm, q0=q0, q1=q1, nq=nq, ti=ti):
                        bank = P.rot("o", [6, 7])
                        P.op("pe", [A("matmul", out=ps[bank][:, hh * 128: hh * 128 + nq],
                                      lhsT=mv[:, half, (hm + hh) * 128:(hm + hh + 1) * 128],
                                      rhs=PT[:, (hh * 2 + half) * 128:(hh * 2 + half) * 128 + nq],
                                      start=(half == 0), stop=(half == 1)) for hh in range(2) for half in range(2)],
                             r=[ptres, ("mv",)], w=[PSB(bank)])
                        P.op("dve", [A("tensor_copy", out=xn[:, 12 + hm + hh, q0:q1], in_=ps[bank][:, hh * 128: hh * 128 + nq])
                                     for hh in range(2)], r=[PSB(bank)], w=[("xn", 12 + hm, ti), ("xn", 13 + hm, ti)])
                    jobs.append(dict(emit_S=emit_S, emit_PV=emit_PV, scale=scale, nsink=None, sink=None, nq=nq))
        attention_jobs(jobs)

    def rope_chunk(bank, n, kb0, out_ap, out_res):
        rb = P.rot("r", [6, 7])
        P.op("act", [A("activation", out=tmpA[:, :n], in_=ps[bank][:, :n], func=AF.Identity)], w=[PSB(bank), "tmpA"])
        P.op("pe", [A("matmul", out=ps[rb][:, :n], lhsT=rrot[:], rhs=tmpA[:, :n], start=True, stop=True)],
             r=["tmpA", "consts"], w=[PSB(rb)])
        P.op("dve", [A("tensor_tensor", out=ps[rb][:, :n], in0=ps[rb][:, :n], in1=sinT[:, kb0:kb0 + n], op=ALU.mult)],
             r=LATE, w=[PSB(rb)])
        P.op("dve", [A("tensor_tensor", out=tmpA[:, :n], in0=tmpA[:, :n], in1=cosT[:, kb0:kb0 + n], op=ALU.mult)],
             r=LATE, w=["tmpA"])
        P.op("dve", [A("tensor_tensor", out=out_ap, in0=tmpA[:, :n], in1=ps[rb][:, :n], op=ALU.add)],
             w=[PSB(rb), "tmpA"] + list(out_res))

    def rope_table_ops():
        if not real:
            return []
        HB = [("B", c, ti) for c in range(8, 16) for ti in range(3)]
        b0 = 4736
        posI = BI32[:, b0:b0 + NKEY]
        kI = posI
        ang = B32[:, b0 + NKEY:b0 + 2 * NKEY]
        tr = B32[:, b0 + 2 * NKEY:b0 + 3 * NKEY]
        kF = B32[:, b0 + 3 * NKEY:b0 + 4 * NKEY]
        C1 = 6.28125
        C2 = TWO_PI - 6.28125
        CL = dict(scalar1=3.1415925, scalar2=-3.1415925, op0=ALU.min, op1=ALU.max)
        dv = lambda ins, r=(): (lambda: P.op("dve", [ins], r=list(r), w=HB))
        return [
            lambda: P.dma(posI, T["pos"], w=HB),
            dv(A("tensor_copy", out=ang, in_=posI)),
            dv(A("tensor_scalar", out=ang, in0=ang, scalar1=invf[:, 0:1], scalar2=None, op0=ALU.mult), r=["consts"]),
            dv(A("tensor_scalar", out=tr, in0=ang, scalar1=1.0 / TWO_PI, scalar2=None, op0=ALU.mult)),
            dv(A("tensor_copy", out=kI, in_=tr)),
            dv(A("tensor_copy", out=kF, in_=kI)),
            dv(A("scalar_tensor_tensor", out=tr, in0=kF, scalar=-C1, in1=ang, op0=ALU.mult, op1=ALU.add)),
            dv(A("scalar_tensor_tensor", out=tr, in0=kF, scalar=-C2, in1=tr, op0=ALU.mult, op1=ALU.add)),
            dv(A("tensor_scalar", out=ang, in0=tr, **CL)),
            lambda: P.op("act", [A("activation", out=sinT, in_=ang, func=AF.Sin)], r=HB, w=LATE),
            dv(A("tensor_scalar", out=tr, in0=tr, scalar1=math.pi / 2, scalar2=None, op0=ALU.add)),
            dv(A("tensor_scalar", out=kF, in0=tr, scalar1=math.pi, scalar2=-TWO_PI, op0=ALU.is_gt, op1=ALU.mult)),
            dv(A("tensor_tensor", out=tr, in0=tr, in1=kF, op=ALU.add)),
            dv(A("tensor_scalar", out=tr, in0=tr, **CL)),
            lambda: P.op("act", [A("activation", out=cosT, in_=tr, func=AF.Sin)], r=HB, w=LATE),
        ]

    def make_kv():
        rmsnorm(12, TT_ALL, h, hres, xn, xres)
        for kvh in range(3):
            wt, wr = P.wtile(("wk", kvh))
            for (ti, lo, hi) in (TT_ALL[0], TT_ALL[1], (0, KOFF, HALO)):
                n = hi - lo
                bank = P.rot("p", PROJ)
                if real:
                    mm_group(ps[bank][:, :n], [(wt[:, kc * 128:(kc + 1) * 128], xn[:, kc, lo:hi]) for kc in range(16)],
                             r=[wr] + [("xn", kc, ti) for kc in range(16)], w=[PSB(bank)])
                    kb0 = lo - KOFF
                    rope_chunk(bank, n, kb0, kT[:, kvh, kb0:kb0 + n], LATE + ["kT"])
        w0, r0 = P.wtile(("wv", 0))
        w1, r1 = P.wtile(("wv", 1))
        if real:
            for blk in range(9):
                b0 = KOFF + blk * 128
                bank = P.rot("p", PROJ)
                tis = sorted({0 if b < HALO else (1 if b < HALO + 512 else 2) for b in (b0, b0 + 127)})
                mm_group(ps[bank][:, 0:192],
                         [(xn[:, kc, b0:b0 + 128], (w0 if kc < 8 else w1)[:, (kc % 8) * 192:(kc % 8 + 1) * 192]) for kc in range(16)],
                         r=[r0, r1] + [("xn", kc, ti) for kc in range(16) for ti in tis], w=[PSB(bank)])
                P.op("act", [A("activation", out=V[:, blk, :], in_=ps[bank][:, 0:192], func=AF.Identity)],
                     w=[PSB(bank), ("V",)])

    def swa_attention(lb):
        jobs = []
        for qb in range(8):
            q0 = HALO + qb * 128
            ti = 1 if qb < 4 else 2
            mi = 1 if qb == 0 else 0
            for hp in range(12):
                kvh = hp // 4

                def emit_S(bank, hp=hp, q0=q0, qb=qb, kvh=kvh, ti=ti, mi=mi):
                    ins = []
                    for hh in range(2):
                        p0 = hh * 64
                        ins.append(A("matmul", out=ps[bank][:, hh * 256:(hh + 1) * 256], lhsT=regB[p0:p0 + 64, hp, q0:q0 + 128],
                                     rhs=kT[p0:p0 + 64, kvh, qb * 128: qb * 128 + 256], start=True, stop=False))
                        ins.append(A("matmul", out=ps[bank][:, hh * 256:(hh + 1) * 256], lhsT=ident[:, :], rhs=maskbf[:, mi, :],
                                     start=False, stop=True))
                    P.op("pe", ins, r=[("B", hp, ti), "kT", "consts"] + LATE, w=[PSB(bank)])

                def emit_PV(PT, ptres, hp=hp, q0=q0, qb=qb, kvh=kvh, ti=ti):
                    bank = P.rot("o", [6, 7])
                    P.op("pe", [A("matmul", out=ps[bank][hh * 64:(hh + 1) * 64, 0:128], lhsT=V[:, qb + half, kvh * 64:(kvh + 1) * 64],
                                  rhs=PT[:, (hh * 2 + half) * 128:(hh * 2 + half + 1) * 128], start=(half == 0), stop=(half == 1))
                                for hh in range(2) for half in range(2)], r=[ptres, ("V",)], w=[PSB(bank)])
                    P.op("dve", [A("tensor_copy", out=xn[:, hp, q0:q0 + 128], in_=ps[bank][:, 0:128])],
                         r=[PSB(bank)], w=[("xn", hp, ti)])
                hd = 2 * hp
                jobs.append(dict(emit_S=emit_S, emit_PV=emit_PV, nq=128, scale=0.125,
                                 nsink=nsinks[:, lb * 24 + hd: lb * 24 + hd + 2],
                                 sink=sinks[:, lb * 24 + hd: lb * 24 + hd + 2]))
        attention_jobs(jobs)

    def mixer(l, tiles):
        pool_layer = l < N_A
        rmsnorm(l * 3 + 1, tiles, h, hres, xn, xres)
        t_lo, t_hi = min(t[1] for t in tiles), max(t[2] for t in tiles)

        def evac(oc, ti, lo, hi, bank):
            n = hi - lo
            if oc >= MIXC:
                P.op("act", [A("activation", out=regB[:, oc, lo:hi], in_=ps[bank][:, :n], func=AF.Identity)],
                     w=[PSB(bank), ("B", oc, ti)])
            elif pool_layer:
                if ti == 0:
                    P.op("act", [A("activation", out=PU[:, lo:hi], in_=ps[bank][:, :n], func=AF.Identity, scale=valid[:, 0:1])],
                         r=["consts"], w=[PSB(bank), "late0"])
                else:
                    P.op("act", [A("activation", out=PU[:, lo:hi], in_=ps[bank][:, :n], func=AF.Identity)],
                         w=[PSB(bank), "late0"])
            else:
                rope_chunk(bank, n, lo - KOFF, regB[:, oc, lo:hi], [("B", oc, ti)])

        def after(oc):
            if not pool_layer or oc >= MIXC:
                return
            gi = oc // 3
            wdw = (2, 4, 8, 16)[gi]
            bufs = {"late0": PU, "late1": PA, "late2": PB}
            cur = "late0"
            sh = 1
            while sh < wdw:
                nxt = "late1" if cur != "late1" else "late2"
                s_, d_ = bufs[cur], bufs[nxt]
                P.op("dve", [A("tensor_tensor", out=d_[:, t_lo + sh:t_hi], in0=s_[:, t_lo + sh:t_hi], in1=s_[:, t_lo:t_hi - sh],
                               op=ALU.add)], r=[cur], w=[nxt])
                P.op("dve", [A("tensor_copy", out=d_[:, t_lo:t_lo + sh], in_=s_[:, t_lo:t_lo + sh])], r=[cur], w=[nxt])
                cur = nxt
                sh *= 2
            S_ = bufs[cur]
            for (ti, lo, hi) in tiles:
                P.op("dve", [A("scalar_tensor_tensor", out=regB[:, oc, lo:hi], in0=S_[:, lo:hi], scalar=1.0 / wdw,
                               in1=PU[:, lo:hi], op0=ALU.mult, op1=ALU.subtract)], r=[cur, "late0"], w=[("B", oc, ti)])
            other = "late2" if cur != "late2" else "late1"
            Tm = bufs[other]
            P.op("dve", [A("tensor_tensor", out=Tm[:, 0:16], in0=S_[:, HALO:HALO + 16], in1=invc[:, gi * 16:(gi + 1) * 16],
                           op=ALU.mult)], r=[cur, "consts"], w=[other])
            P.op("dve", [A("tensor_tensor", out=regB[:, oc, HALO:HALO + 16], in0=Tm[:, 0:16], in1=PU[:, HALO:HALO + 16],
                           op=ALU.subtract)], r=[other, "late0"], w=[("B", oc, 1)])

        proj_tiles("win", lambda oc: (l, oc), 16, tiles, evac, after, head=(0 if pool_layer else 4))

        if pool_layer:
            for gi in range(4):
                wt, wr = P.wtile(("poolw", l, gi))
                for oc3 in range(3):
                    oc = gi * 3 + oc3
                    for (ti, lo, hi) in tiles:
                        n = hi - lo
                        bank = P.rot("p", PROJ)
                        if real:
                            mm_group(ps[bank][:, :n],
                                     [(wt[:, kc3 * 384 + oc3 * 128: kc3 * 384 + (oc3 + 1) * 128], regB[:, gi * 3 + kc3, lo:hi])
                                      for kc3 in range(3)],
                                     r=[wr] + [("B", gi * 3 + kc3, ti) for kc3 in range(3)], w=[PSB(bank)])
                            P.op("act", [A("activation", out=xn[:, oc, lo:hi], in_=ps[bank][:, :n], func=AF.Identity,
                                           scale=psc[:, l * 12 + oc: l * 12 + oc + 1])],
                                 r=["consts"], w=[PSB(bank), ("xn", oc, ti)])
        elif real:
            swa_attention(l - N_A)
        if real:
            mem_attention(tiles)

        def evac_out(oc, ti, lo, hi, bank):
            n = hi - lo
            P.op("dve", [A("tensor_tensor", out=h[:, oc, lo:hi], in0=ps[bank][:, :n], in1=h[:, oc, lo:hi], op=ALU.add)],
                 w=[PSB(bank), ("h", oc, ti)])
        proj_tiles("wout", lambda oc: (l, oc), 16, tiles, evac_out)

    def dump(k):
        if real and dbg is not None and k < dbg:
            P.dma(T["dbg"][k].rearrange("(c p) t -> p c t", p=128), h[:, :, :],
                  r=[("h", c, ti) for c in range(16) for ti in range(3)], w=[("dbgout", k)])
            P.final.append(("dbgout", k))

    k = 0
    for l in range(nlayers):
        tiles = TT_ALL if l < N_A else TT_MAIN
        ffn(l, 0, tiles)
        dump(k); k += 1
        mixer(l, tiles)
        dump(k); k += 1
        ffn(l, 1, tiles, extra=(rope_table_ops() if (l == N_A - 1 and nlayers > N_A) else ()))
        dump(k); k += 1
        if l == N_A - 1 and nlayers > N_A:
            make_kv()
    rmsnorm(14, TT_MAIN, h, hres, h, hres)
    if real:
        outT = T["outT"]
        for (ti, lo, hi) in TT_MAIN:
            for q in range(4):
                P.dma(outT[4 * q * 128:(4 * q + 4) * 128, lo - HALO:hi - HALO].rearrange("(c p) t -> p c t", p=128),
                      h[:, 4 * q:4 * q + 4, lo:hi], r=[("h", c, ti) for c in range(4 * q, 4 * q + 4)], w=[("out", q, ti)],
                      eng=("act" if q % 2 else "sp"))
                P.final.append(("out", q, ti))


def plan_counts(plan):
    cnt = {}
    for p in plan:
        cnt[p[0]] = cnt.get(p[0], 0) + 1
    return cnt


def build(nlayers=DEPTH, dbg=None):
    P0 = Prog(None, None, None, None, None)
    emit_program(P0, {k: None for k in (
        "h xn regB late ps tmpA scr V mkT mv gains psc sinks nsinks invf ident rrot ones invc valid epst att "
        "B32 BI32 cosT sinT kT PU PA PB memn maskbf m3b").split()}, nlayers, dbg)
    plan = P0.plan

    nc = bass.Bass("TRN2", target_bir_lowering=False)
    dt = lambda name, shape, dtype=F32, kind="ExternalInput": nc.dram_tensor(name, shape, dtype, kind=kind).ap()
    NL, NP = nlayers, min(nlayers, N_A)
    wsrc = {
        "gate": dt("w_gate", [NL, 2, NFF, 128, 2048]),
        "up": dt("w_up", [NL, 2, NFF, 128, 2048]),
        "down": dt("w_down", [NL, 2, NGRP, 4, 128, 2048]),
        "win": dt("w_in", [NL, 16, 128, 2048]),
        "wout": dt("w_out", [NL, 16, 128, 2048]),
        "poolw": dt("w_pool", [NP, 4, 128, 2048]),
        "memk": dt("w_memk", [NL, 4, 128, 2048]),
        "memv": dt("w_memv", [NL, 4, 128, 2048]),
        "wk": dt("w_k", [3, 128, 2048]),
        "wv": dt("w_v", [2, 128, 2048]),
    }
    T = {}
    T["xT"] = dt("xT", [D, TB])
    T["cst"] = dt("cst", [128, C_N])
    T["ccst"] = dt("ccst", [128, CC_N])
    T["memT"] = dt("memT", [D, 256])
    T["pos"] = dt("pos", [128, NKEY], I32)
    T["outT"] = dt("outT", [D, TOK], F32, "ExternalOutput")
    if dbg is not None:
        T["dbg"] = dt("dbg", [dbg, D, TB], F32, "ExternalOutput")

    es = ExitStack()
    with es:
        E = es.enter_context
        sb = lambda name, shape, dtype: E(nc.sbuf_tensor(name, shape, dtype))
        T["h"] = sb("h", [128, 16, TB], F32)
        T["xn"] = sb("xn", [128, 16, TB], BF16)
        T["regB"] = sb("regB", [128, 16, TB], BF16)
        ring = sb("ring", [128, R, 2048], BF16)
        T["late"] = sb("late", [128, 4032], F32)
        T["tmpA"] = sb("tmpA", [128, 512], F32)
        T["V"] = sb("V", [128, 9, 192], BF16)
        T["mkT"] = sb("mkT", [128, 4, 256], BF16)
        T["mv"] = sb("mv", [128, 2, 512], BF16)
        T["gains"] = sb("gains", [128, 240], F32)
        T["psc"] = sb("psc", [128, 24], F32)
        T["sinks"] = sb("sinks", [128, 48], F32)
        T["nsinks"] = sb("nsinks", [128, 48], F32)
        T["invf"] = sb("invf", [128, 1], F32)
        T["maskbf"] = sb("maskbf", [128, 2, 256], BF16)
        T["ident"] = sb("ident", [128, 128], BF16)
        T["rrot"] = sb("rrot", [128, 128], F32)
        T["ones"] = sb("ones", [128, 128], BF16)
        T["invc"] = sb("invc", [128, 64], F32)
        T["valid"] = sb("valid", [128, 1], F32)
        T["epst"] = sb("epst", [128, 1], F32)
        T["att"] = {
            "stat": sb("a_stat", [128, 48], F32),
            "P": sb("a_P", [128, 3, 512], BF16),
            "D": sb("a_D", [128, 3, 2, 128], BF16),
            "PT": sb("a_PT", [128, 2, 512], BF16),
        }
        T["ps"] = [E(nc.psum_tensor("ps%d" % i, [128, 512], F32)) for i in range(8)]
        Bflat = T["regB"][:].rearrange("p a b -> p (a b)")
        T["B32"] = Bflat.bitcast(F32)
        T["BI32"] = Bflat.bitcast(I32)
        T["m3b"] = Bflat[:, 9472:13568].rearrange("p (c m) -> p c m", m=256)
        T["memn"] = Bflat[:, 13568:17664].rearrange("p (c m) -> p c m", m=256)
        lt = T["late"]
        T["PU"], T["PA"], T["PB"] = lt[:, 0:TB], lt[:, TB:2 * TB], lt[:, 2 * TB:3 * TB]
        T["cosT"], T["sinT"] = lt[:, 0:NKEY], lt[:, NKEY:2 * NKEY]
        T["kT"] = lt[:, 2 * NKEY:4032].bitcast(BF16).rearrange("p (k t) -> p k t", t=NKEY)
        T["scr"] = T["att"]["P"][:].rearrange("p a b -> p (a b)")
        P = Prog(nc, es, plan, ring, wsrc)
        emit_program(P, T, nlayers, dbg)
        assert P.wi == len(plan)
        block = E(nc.Block())
        P.replay(block)
    return nc, plan


def _tile_cols(W, ncol_chunks):
    K = W.shape[0] // 128
    return np.ascontiguousarray(W.reshape(K, 128, ncol_chunks, 128).transpose(2, 1, 0, 3)).reshape(ncol_chunks, 128, K * 128)


def prepare_shared(inp, NL=DEPTH):
    f = np.float32
    sh = {}
    NP = min(NL, N_A)
    g = np.empty((NL, 2, NFF, 128, 2048), f)
    u = np.empty((NL, 2, NFF, 128, 2048), f)
    dn = np.empty((NL, 2, NGRP, 4, 128, 2048), f)
    for l in range(NL):
        for j in range(2):
            g[l, j] = _tile_cols(inp["ffn_gate"][l, j], NFF)
            u[l, j] = _tile_cols(inp["ffn_up"][l, j], NFF)
            Wd = inp["ffn_down"][l, j]
            dn[l, j] = np.ascontiguousarray(Wd.reshape(NGRP, 4, 128, 4, 512).transpose(0, 3, 2, 1, 4)).reshape(NGRP, 4, 128, 2048)
    sh["w_gate"], sh["w_up"], sh["w_down"] = g, u, dn
    sh["w_in"] = np.stack([_tile_cols(inp["w_in"][l], 16) for l in range(NL)])
    sh["w_out"] = np.stack([_tile_cols(inp["w_out"][l], 16) for l in range(NL)])
    pw = np.zeros((NP, 4, 128, 2048), f)
    for l in range(NP):
        for gi in range(4):
            pw[l, gi, :, :1152] = inp["pool_w"][l, gi].reshape(3, 128, 384).transpose(1, 0, 2).reshape(128, 1152)
    sh["w_pool"] = pw
    sh["w_memk"] = np.stack([_tile_cols(inp["w_mem_kv"][l][:, :512], 4) for l in range(NL)])
    mvv = np.empty((NL, 4, 128, 2048), f)
    for l in range(NL):
        Wv = inp["w_mem_kv"][l][:, 512:]
        mvv[l] = Wv.reshape(4, 4, 128, 512).transpose(0, 2, 1, 3).reshape(4, 128, 2048)
    sh["w_memv"] = mvv
    wk = inp["w_kv"][:, :192]
    wkd = np.concatenate([np.concatenate([wk[:, hh * 64:(hh + 1) * 64]] * 2, axis=1) for hh in range(3)], axis=1)
    sh["w_k"] = _tile_cols(np.ascontiguousarray(wkd), 3)
    wv = np.zeros((2, 128, 2048), f)
    wv[:, :, :1536] = inp["w_kv"][:, 192:].reshape(2, 8, 128, 192).transpose(0, 2, 1, 3).reshape(2, 128, 1536)
    sh["w_v"] = wv
    sh["memT"] = np.ascontiguousarray(inp["mem"][0].T)
    cst = np.zeros((128, C_N), f)
    vecs = np.concatenate([inp["norms"].reshape(12, D), inp["kv_norm"][None], inp["mem_norm"][None], inp["final_norm"][None]], 0)
    cst[:, C_GAIN:C_GAIN + 240] = vecs.reshape(15, 16, 128).transpose(2, 0, 1).reshape(128, 240)
    cst[:, C_PSC:C_PSC + 24] = inp["pool_scale"].reshape(2, 12, 128).transpose(2, 0, 1).reshape(128, 24)
    cst[:, C_SINK:C_SINK + 48] = np.broadcast_to(inp["sinks"].reshape(1, 48), (128, 48))
    half = 32
    invfreq = (1.0 / (np.float32(10000.0) ** (np.arange(half, dtype=f) * f(2.0 / 64)))).astype(f)
    cst[:, C_INVF] = invfreq[np.arange(128) % 32]
    qi = np.arange(128)[:, None] + 128
    kj = np.arange(256)[None, :]
    rel = qi - kj
    band = (rel >= 0) & (rel < 128)
    cst[:, C_MASK:C_MASK + 256] = np.where(band, 0.0, MASKV)
    cst[:, C_ID:C_ID + 128] = np.eye(128, dtype=f)
    rr = np.zeros((128, 128), f)
    for m in range(128):
        b, i = (m // 64) * 64, m % 64
        if i < 32:
            rr[b + i + 32, m] = -1.0
        else:
            rr[b + i - 32, m] = 1.0
    cst[:, C_ROT:C_ROT + 128] = rr
    sh["cst"] = cst
    sh["_band"] = band
    return sh


def prepare_core(inp, sh, ci):
    f = np.float32
    s0 = ci * TOK
    x = inp["x"][0]
    xT = np.zeros((D, TB), f)
    lo = max(0, s0 - HALO)
    xT[:, TB - (s0 + TOK - lo):] = x[lo:s0 + TOK].T
    pos = np.zeros((NKEY,), np.int32)
    plo = max(0, s0 - 128)
    pos[NKEY - (s0 + TOK - plo):] = inp["positions"][0, plo:s0 + TOK]
    cc = np.zeros((128, CC_N), f)
    for gi, w in enumerate((2, 4, 8, 16)):
        if ci == 0:
            cc[:, CC_INVC + gi * 16: CC_INVC + (gi + 1) * 16] = 1.0 / np.minimum(np.arange(16) + 1.0, float(w))
        else:
            cc[:, CC_INVC + gi * 16: CC_INVC + (gi + 1) * 16] = 1.0 / w
    cc[:, CC_VALID] = 0.0 if ci == 0 else 1.0
    band = sh["_band"].copy()
    if ci == 0:
        band[:, :128] = False
    cc[:, CC_MASK0:CC_MASK0 + 256] = np.where(band, 0.0, MASKV)
    return {"xT": xT, "pos": np.ascontiguousarray(np.broadcast_to(pos[None, :], (128, NKEY))), "ccst": cc}


_CACHE = {}


def run(inp, nlayers=DEPTH, dbg=None, cores=NCORES, trace=False):
    key = (nlayers, dbg)
    if key not in _CACHE:
        _CACHE[key] = build(nlayers, dbg)
    nc, plan = _CACHE[key]
    sh = prepare_shared(inp, nlayers)
    shared = {k: v for k, v in sh.items() if not k.startswith("_")}
    in_maps = []
    for ci in range(cores):
        m = dict(shared)
        m.update(prepare_core(inp, sh, ci))
        in_maps.append(m)
    res = run_bass_kernel_spmd(nc, in_maps, core_ids=list(range(cores)), trace=trace)
    return res


def kernel(**inputs):
    inp = {k: np.asarray(v) for k, v in inputs.items()}
    res = run(inp)
    out = np.empty((1, SEQ, D), np.float32)
    for ci in range(NCORES):
        out[0, ci * TOK:(ci + 1) * TOK, :] = res.results[ci]["outT"].T
    return out
```
